# Optimizing a Trainium2 kernel written in Bass

```python
import math
import jax, jax.numpy as jnp
from jax import lax
import numpy as np

D_MODEL = 1024
BATCH = 16
SEQ = 2048
DEPTH = 1

CHUNK = 64
N_META = 16
Q_BLOCK = 128
SB_HEADS = 8
SB_HEAD_DIM = 64
SB_WIDTH = SB_HEADS * SB_HEAD_DIM
MLA_HEADS = 8
MLA_Q_LORA = 384
MLA_KV_LORA = 256
MLA_NOPE = 64
MLA_ROPE = 32
MLA_V = 64
MLA_WIDTH = MLA_HEADS * MLA_V
ROPE_THETA = 10000.0
N_BRANCH = 2
IN_COLS = 3 * SB_WIDTH + MLA_Q_LORA + MLA_KV_LORA + MLA_ROPE + N_BRANCH * D_MODEL
D_FF = 2816
EPS = 1e-6
NEG_INF = -1e30
PAD_CHUNK_ID = 2 ** 30

kernel_name = "hybrid_sb_mla_macaron_block"


def rmsnorm(x, g):
    xf = x.astype(jnp.float32)
    y = xf * lax.rsqrt(jnp.mean(xf * xf, axis=-1, keepdims=True) + EPS)
    return (y * g.astype(jnp.float32)).astype(x.dtype)


def swiglu(h, w_gate, w_up, w_down):
    return (jax.nn.silu(h @ w_gate) * (h @ w_up)) @ w_down


def rope(x, pos):
    half = x.shape[-1] // 2
    inv = ROPE_THETA ** (-jnp.arange(half, dtype=jnp.float32) / half)
    ang = pos[:, None] * inv[None, :]
    cos = jnp.cos(ang).astype(x.dtype)
    sin = jnp.sin(ang).astype(x.dtype)
    x1, x2 = x[..., :half], x[..., half:]
    return jnp.concatenate([x1 * cos - x2 * sin, x1 * sin + x2 * cos], axis=-1)


def to_heads(t, n_heads):
    b, l, _ = t.shape
    return t.reshape(b, l, n_heads, -1).transpose(0, 2, 1, 3)


def from_heads(t):
    b, h, l, d = t.shape
    return t.transpose(0, 2, 1, 3).reshape(b, l, h * d)


def chunk_end(pos):
    if pos < N_META:
        return N_META
    return N_META + ((pos - N_META) // CHUNK + 1) * CHUNK


def stick_breaking_attention(q, k, v):
    lp, d = q.shape[2], q.shape[3]
    scale = d ** -0.5
    outs = []
    for blk in range(lp // Q_BLOCK):
        q0, q1 = blk * Q_BLOCK, (blk + 1) * Q_BLOCK
        z = jnp.einsum('bhtd,bhsd->bhts', q[:, :, q0:q1], k[:, :, :q1]).astype(jnp.float32) * scale
        strict = jnp.arange(q1)[None, :] < jnp.arange(q0, q1)[:, None]
        log_1m = jnp.where(strict, jax.nn.log_sigmoid(-z), 0.0)
        csum = jnp.cumsum(log_1m, axis=-1)
        log_a = jax.nn.log_sigmoid(z) + csum[..., -1:] - csum
        a = jnp.where(strict, jnp.exp(log_a), 0.0)
        outs.append(jnp.einsum('bhts,bhsd->bhtd', a.astype(v.dtype), v[:, :, :q1]))
    return jnp.concatenate(outs, axis=2)


def latent_attention(q_nope, q_rope, k_nope, k_rope, v, cid):
    lp = q_nope.shape[2]
    scale = (MLA_NOPE + MLA_ROPE) ** -0.5
    outs = []
    for blk in range(lp // Q_BLOCK):
        q0, q1 = blk * Q_BLOCK, (blk + 1) * Q_BLOCK
        kend = min(lp, chunk_end(q1 - 1))
        s = (jnp.einsum('bhtd,bhsd->bhts', q_nope[:, :, q0:q1], k_nope[:, :, :kend])
             + jnp.einsum('bhtd,bsd->bhts', q_rope[:, :, q0:q1], k_rope[:, :kend]))
        s = s.astype(jnp.float32) * scale
        mask = cid[None, :kend] <= cid[q0:q1, None]
        p = jax.nn.softmax(jnp.where(mask, s, NEG_INF), axis=-1)
        outs.append(jnp.einsum('bhts,bhsd->bhtd', p.astype(v.dtype), v[:, :, :kend]))
    return jnp.concatenate(outs, axis=2)


def hybrid_mixer(u, w_in, b_gate, q_norm_g, w_uq, kv_norm_g, w_ukv, w_sb_o, w_mla_o, w_out):
    bsz, l, _ = u.shape
    lp = -(-l // Q_BLOCK) * Q_BLOCK
    up = jnp.pad(u, ((0, 0), (0, lp - l), (0, 0)))
    proj = up @ w_in
    sizes = [SB_WIDTH, SB_WIDTH, SB_WIDTH, MLA_Q_LORA, MLA_KV_LORA, MLA_ROPE, N_BRANCH * D_MODEL]
    idx = np.cumsum(sizes)[:-1].tolist()
    sb_q, sb_k, sb_v, c_q, c_kv, k_r, gate_pre = jnp.split(proj, idx, axis=-1)

    pos = jnp.arange(lp)
    cid = jnp.where(pos < N_META, 0, 1 + (pos - N_META) // CHUNK)
    cid = jnp.where(pos < l, cid, PAD_CHUNK_ID)
    posf = pos.astype(jnp.float32)

    y_sb = stick_breaking_attention(to_heads(sb_q, SB_HEADS), to_heads(sb_k, SB_HEADS),
                                    to_heads(sb_v, SB_HEADS))
    y_sb = from_heads(y_sb) @ w_sb_o

    q = jnp.einsum('bld,dhe->bhle', rmsnorm(c_q, q_norm_g), w_uq)
    q_nope, q_rope = q[..., :MLA_NOPE], rope(q[..., MLA_NOPE:], posf)
    kv = jnp.einsum('bld,dhe->bhle', rmsnorm(c_kv, kv_norm_g), w_ukv)
    k_nope, v = kv[..., :MLA_NOPE], kv[..., MLA_NOPE:]
    k_rope = rope(k_r, posf)
    y_mla = latent_attention(q_nope, q_rope, k_nope, k_rope, v, cid)
    y_mla = from_heads(y_mla) @ w_mla_o

    g = jax.nn.sigmoid(gate_pre + b_gate)
    g_sb, g_mla = g[..., :D_MODEL], g[..., D_MODEL:]
    out = (g_sb * y_sb + g_mla * y_mla) @ w_out
    return out[:, :l]


def setup_inputs(seed: int = 0) -> dict:
    key = jax.random.key(seed)
    ks = jax.random.split(key, 24)

    def w(k, shape, fan_in):
        return jax.random.normal(k, shape, jnp.float32) * fan_in ** -0.5

    def gain(k, n):
        return 1.0 + 0.02 * jax.random.normal(k, (DEPTH, n), jnp.float32)

    return {
        "x": jax.random.normal(ks[0], (BATCH, SEQ, D_MODEL), jnp.float32),
        "meta_tokens": jax.random.normal(ks[1], (N_META, D_MODEL), jnp.float32),
        "ffn1_pre_g": gain(ks[2], D_MODEL),
        "ffn1_w_gate": w(ks[3], (DEPTH, D_MODEL, D_FF), D_MODEL),
        "ffn1_w_up": w(ks[4], (DEPTH, D_MODEL, D_FF), D_MODEL),
        "ffn1_w_down": w(ks[5], (DEPTH, D_FF, D_MODEL), D_FF),
        "ffn1_post_g": gain(ks[6], D_MODEL),
        "mix_pre_g": gain(ks[7], D_MODEL),
        "w_in": w(ks[8], (DEPTH, D_MODEL, IN_COLS), D_MODEL),
        "b_gate": 0.02 * jax.random.normal(ks[9], (DEPTH, N_BRANCH * D_MODEL), jnp.float32),
        "q_norm_g": gain(ks[10], MLA_Q_LORA),
        "w_uq": w(ks[11], (DEPTH, MLA_Q_LORA, MLA_HEADS, MLA_NOPE + MLA_ROPE), MLA_Q_LORA),
        "kv_norm_g": gain(ks[12], MLA_KV_LORA),
        "w_ukv": w(ks[13], (DEPTH, MLA_KV_LORA, MLA_HEADS, MLA_NOPE + MLA_V), MLA_KV_LORA),
        "w_sb_o": w(ks[14], (DEPTH, SB_WIDTH, D_MODEL), SB_WIDTH),
        "w_mla_o": w(ks[15], (DEPTH, MLA_WIDTH, D_MODEL), MLA_WIDTH),
        "w_out": w(ks[16], (DEPTH, D_MODEL, D_MODEL), D_MODEL),
        "mix_post_g": gain(ks[17], D_MODEL),
        "ffn2_pre_g": gain(ks[18], D_MODEL),
        "ffn2_w_gate": w(ks[19], (DEPTH, D_MODEL, D_FF), D_MODEL),
        "ffn2_w_up": w(ks[20], (DEPTH, D_MODEL, D_FF), D_MODEL),
        "ffn2_w_down": w(ks[21], (DEPTH, D_FF, D_MODEL), D_FF),
        "ffn2_post_g": gain(ks[22], D_MODEL),
    }


def reference(x, meta_tokens, ffn1_pre_g, ffn1_w_gate, ffn1_w_up, ffn1_w_down, ffn1_post_g,
              mix_pre_g, w_in, b_gate, q_norm_g, w_uq, kv_norm_g, w_ukv, w_sb_o, w_mla_o,
              w_out, mix_post_g, ffn2_pre_g, ffn2_w_gate, ffn2_w_up, ffn2_w_down, ffn2_post_g):
    bsz = x.shape[0]
    meta = jnp.broadcast_to(meta_tokens[None].astype(x.dtype), (bsz, N_META, x.shape[-1]))
    h = jnp.concatenate([meta, x], axis=1)
    for l in range(DEPTH):
        f = swiglu(rmsnorm(h, ffn1_pre_g[l]), ffn1_w_gate[l], ffn1_w_up[l], ffn1_w_down[l])
        h = h + 0.5 * rmsnorm(f, ffn1_post_g[l])
        m = hybrid_mixer(rmsnorm(h, mix_pre_g[l]), w_in[l], b_gate[l], q_norm_g[l], w_uq[l],
                         kv_norm_g[l], w_ukv[l], w_sb_o[l], w_mla_o[l], w_out[l])
        h = h + rmsnorm(m, mix_post_g[l])
        f = swiglu(rmsnorm(h, ffn2_pre_g[l]), ffn2_w_gate[l], ffn2_w_up[l], ffn2_w_down[l])
        h = h + 0.5 * rmsnorm(f, ffn2_post_g[l])
    return h[:, N_META:]
```

```python
import numpy as np
from contextlib import ExitStack
import concourse.bass as bass
import concourse.mybir as mybir
from concourse.bass_utils import run_bass_kernel_spmd

F32 = mybir.dt.float32
BF16 = mybir.dt.bfloat16
AF = mybir.ActivationFunctionType
ALU = mybir.AluOpType

NCORES = 8
D = 1024
DFF = 2816
NFF = DFF // 128
SEQ = 2048
NMETA = 16
SEQ_PER_CORE = 2
XTOK = SEQ_PER_CORE * SEQ
G = 512
NG = XTOK // G
GPS = SEQ // G
KLEN = NMETA + SEQ
NVB = 1 + SEQ // 128
IN_COLS = 4256
EPS = 1e-6
MLA_SCALE = 96.0 ** -0.5
NSLOT = 6
CFG = {
    "sb_stages": [["Z"], ["A"], [], ["A2"], [], ["B"], ["C"], ["D"], ["E"]],
    "mla_stages": [["A"], ["B"], [], ["C"]],
    "order": "rev",
    "NE": 3, "NSP": 3, "NS": 2, "NA": 3, "NT": 3, "NP": 3,
}
SLOT_ELEMS = 2048


class Op:
    __slots__ = ("eng", "fn", "deps", "idx", "signal", "dma", "sigcount", "waits", "cost", "seq", "tag")


class Sched:
    ENGS = ("pe", "act", "dve", "pool", "sp")
    curtag = ""

    def __init__(self):
        self.ops = {e: [] for e in self.ENGS}
        self.lastw = {}
        self.readers = {}
        self.dmacnt = {}
        self.nseq = 0

    def simulate(self, semlat=120.0):
        allops = sorted((o for e in self.ENGS for o in self.ops[e]), key=lambda o: o.seq)
        t_eng = {e: 0.0 for e in self.ENGS}
        busy = {e: 0.0 for e in self.ENGS}
        done = {}
        self.dma_free = 0.0
        self.dma_busy = 0.0
        pred = {}
        starts = {}
        last_on = {}
        for op in allops:
            ready = 0.0
            rd = None
            for d in op.deps:
                if d.eng == "pe" and op.eng == "pe":
                    continue
                if done[id(d)] + semlat > ready:
                    ready = done[id(d)] + semlat
                    rd = d
            start = max(t_eng[op.eng], ready)
            if op.dma is not None:
                occ, nbytes = op.cost
                t_eng[op.eng] = start + occ
                busy[op.eng] += occ
                xs = max(start + occ, self.dma_free)
                self.dma_free = xs + nbytes / 300.0
                self.dma_busy += nbytes / 300.0
                done[id(op)] = self.dma_free + 2000.0
                pred[id(op)] = ("dep", rd) if ready > start - 1e-9 else ("eng", last_on.get(op.eng))
                last_on[op.eng] = op
                starts[id(op)] = start
                continue
            if ready > t_eng[op.eng]:
                pred[id(op)] = ("dep", rd)
            else:
                pred[id(op)] = ("eng", last_on.get(op.eng))
            last_on[op.eng] = op
            starts[id(op)] = start
            occ, lat = op.cost
            t_eng[op.eng] = start + occ
            busy[op.eng] += occ
            done[id(op)] = start + occ + lat
        self.sim_done = done
        self.sim_pred = pred
        self.sim_start = starts
        return max(done.values()), busy

    def add(self, eng, fn, reads=(), writes=(), dma=None, nodep=(), cost=None):
        op = Op()
        op.cost = cost if cost is not None else (600.0, 0.0)
        op.seq = self.nseq
        self.nseq += 1
        op.tag = Sched.curtag
        op.eng = eng
        op.fn = fn
        op.signal = False
        op.sigcount = 0
        op.dma = None
        deps = {}
        for r in reads:
            w = self.lastw.get(r)
            if w is not None:
                deps[id(w)] = w
            if r[0] == "ps":
                for rd in self.readers.get(r, ()):
                    if rd.eng != eng:
                        deps[id(rd)] = rd
        for r in writes:
            w = self.lastw.get(r)
            if w is not None:
                deps[id(w)] = w
            for rd in self.readers.get(r, ()):
                deps[id(rd)] = rd
        for r in reads:
            self.readers.setdefault(r, []).append(op)
        for r in writes:
            self.lastw[r] = op
            self.readers[r] = []
        for n in nodep:
            deps.pop(id(n), None)
        deps.pop(id(op), None)
        op.deps = list(deps.values())
        if dma is not None:
            c = self.dmacnt.get(dma, 0) + 16
            self.dmacnt[dma] = c
            op.dma = (dma, c)
        op.idx = len(self.ops[eng])
        self.ops[eng].append(op)
        return op

    def finalize(self):
        for eng in self.ENGS:
            wm = {}
            dwm = {}
            for op in self.ops[eng]:
                need = {}
                dwaits = []
                for d in op.deps:
                    if d.dma is not None:
                        name, val = d.dma
                        if dwm.get(name, 0) < val:
                            dwm[name] = val
                            dwaits.append((name, val))
                    else:
                        if d.eng == "pe" and eng == "pe":
                            continue
                        if wm.get(d.eng, -1) < d.idx:
                            if d.eng not in need or need[d.eng].idx < d.idx:
                                need[d.eng] = d
                for e2, d in need.items():
                    wm[e2] = d.idx
                    d.signal = True
                op.waits = (list(need.values()), dwaits)
        for eng in self.ENGS:
            cnt = 0
            for op in self.ops[eng]:
                if op.signal and op.dma is None:
                    cnt += 1
                    op.sigcount = cnt

    def run(self, e, eng, esem, dsem):
        for op in self.ops[eng]:
            for d in op.waits[0]:
                e.wait_ge(esem[d.eng], d.sigcount)
            for (name, val) in op.waits[1]:
                e.wait_ge(dsem[name], val)
            if op.fn is None:
                continue
            ins = op.fn(e)
            if op.dma is not None:
                ins.then_inc(dsem[op.dma[0]], 16)
            elif op.signal:
                ins.then_inc(esem[eng], 1)


def build_nc(dbg=None):
    nc = bass.Bass("TRN2", target_bir_lowering=False)

    def din(name, shape):
        return nc.dram_tensor(name, list(shape), F32, kind="ExternalInput").ap()

    x_d = din("x", [XTOK, D])
    meta_d = din("meta", [NMETA, D])
    gains_pre_d = din("gains_pre", [3, D])
    gains_post_d = din("gains_post", [1, 3 * D])
    wg_d = [din("ffn1_w_gate", [D, DFF]), din("ffn2_w_gate", [D, DFF])]
    wu_d = [din("ffn1_w_up", [D, DFF]), din("ffn2_w_up", [D, DFF])]
    wd_d = [din("ffn1_w_down", [DFF, D]), din("ffn2_w_down", [DFF, D])]
    win_d = din("w_in", [D, IN_COLS])
    bgate_d = din("b_gate", [1, 2 * D])
    qg_d = din("q_norm_g", [1, 384])
    kvg_d = din("kv_norm_g", [1, 256])
    wuq_d = din("w_uq", [384, 8, 96])
    wukv_d = din("w_ukv", [256, 8, 128])
    wsbo_d = din("w_sb_o", [512, D])
    wmlao_d = din("w_mla_o", [512, D])
    wout_d = din("w_out", [D, D])
    rope_d = din("rope_tab", [2, 96, KLEN])
    out_d = nc.dram_tensor("out", [XTOK, D], F32, kind="ExternalOutput").ap()

    S = Sched()
    es = ExitStack()
    with es:
        def sb(name, shape, dt=F32):
            return es.enter_context(nc.sbuf_tensor(name, list(shape), dt))

        ident = sb("ident", [128, 128], BF16)
        ones_bf = sb("ones_bf", [128, 128], BF16)
        ones_f = sb("ones_f", [128, 512], F32)
        maskc = sb("maskc", [128, 128], F32)
        negm = sb("negm", [128, 128], BF16)
        negm2 = sb("negm2", [128, 128], BF16)
        gpre = sb("gpre", [128, 3, 8], F32)
        gpost1 = sb("gpost1", [128, D], F32)
        bg = sb("bg", [128, 16], F32)
        qg = sb("qg", [128, 3], F32)
        kvg = sb("kvg", [128, 2], F32)
        cosT = sb("cosT", [96, G], F32)
        sinT = sb("sinT", [96, G], F32)
        h = sb("h", [128, 4, D], F32)
        xn = sb("xn", [128, D], BF16)
        xn2 = [xn, xn]
        stat = sb("stat", [128, 32], F32)
        uT = sb("uT", [128, 8, G], BF16)
        actT = sb("actT", [128, NFF, G], BF16)
        slots = [sb(f"slot{i}", [128, SLOT_ELEMS], BF16) for i in range(NSLOT)]
        qm_sb = sb("qm_sb", [128, 8, G], BF16)
        kT_sb = sb("kT_sb", [128, 4, KLEN], BF16)
        v_sb = sb("v_sb", [128, NVB, 512], BF16)
        qm_nope = sb("qm_nope", [128, 8, G], BF16)
        qm_rope = sb("qm_rope", [96, 8, G], BF16)
        kT_nope = sb("kT_nope", [128, 4, KLEN], BF16)
        kT_rope = sb("kT_rope", [96, KLEN], BF16)
        v_mla = sb("v_mla", [128, NVB, 8, 65], BF16)
        NE, NS, NA, NT_, NP, NSP = CFG["NE"], CFG["NS"], CFG["NA"], CFG["NT"], CFG["NP"], CFG["NSP"]
        EB = sb("EB", [128, NE, 512], F32)
        Ebuf = [EB[:, i, :] for i in range(NE)]
        ptmp = EB[:, 0:2, :].rearrange("p a b -> p (a b)")
        identf = EB[:, 0, 0:128]
        mt1 = Ebuf[0]
        mt2 = Ebuf[1]
        Sfx = [sb(f"Sfx{i}", [128, 512], F32) for i in range(NS)]
        SPB = sb("SPB", [128, NSP, 512], F32)
        SPbuf = [SPB[:, i, :] for i in range(NSP)]
        ptmp2 = SPB[:, 0:2, :].rearrange("p a b -> p (a b)")
        abuf = [sb(f"abuf{i}", [128, 512], BF16) for i in range(NA)]
        aT = [sb(f"aT{i}", [128, 4, 128], BF16) for i in range(NT_)]
        pT = [sb(f"pT{i}", [128, 512], BF16) for i in range(NP)]
        ytok = sb("ytok", [128, 512], BF16)
        ytok2 = sb("ytok2", [128, 512], BF16)
        rc = sb("rc", [128, 8], F32)
        mixT = actT[:, 0:8, :]
        yT_sb = actT[:, 8:12, :]
        yT_mla = actT[:, 12:16, :]
        cqT = actT[:, 16:19, :]
        ckvT = actT[:, 19:21, :]
        K_ysb = [("actT", k) for k in range(8, 12)]
        K_ymla = [("actT", k) for k in range(12, 16)]
        K_cq = [("actT", k) for k in range(16, 19)]
        K_ckv = [("actT", k) for k in range(19, 21)]
        rt1 = Sfx[0][0:96, :]
        rt2 = Sfx[1][0:96, :]
        sg = [abuf[0], abuf[1]]
        gs = pT[0]
        gm = pT[1]
        rq_rep = Ebuf[0]
        rkv_rep = Ebuf[1]
        ps = es.enter_context(nc.psum_tensor("ps", [128, 8, 512], F32))

        csem = es.enter_context(nc.semaphore("csem"))
        psem = es.enter_context(nc.semaphore("psem"))
        with nc.Block() as block:
            @block.sync
            def _(e):
                n = 0
                def col(dst, src_row, c):
                    return e.dma_start(out=dst, in_=src_row[:, c * 128:(c + 1) * 128].rearrange("a p -> p a"),
                                       allow_slow_non_contiguous=True)
                for a in range(3):
                    for c in range(8):
                        col(gpre[:, a, c:c + 1], gains_pre_d[a:a + 1, :], c).then_inc(csem, 16); n += 16
                for c in range(16):
                    col(bg[:, c:c + 1], bgate_d, c).then_inc(csem, 16); n += 16
                for c in range(3):
                    col(qg[:, c:c + 1], qg_d, c).then_inc(csem, 16); n += 16
                for c in range(2):
                    col(kvg[:, c:c + 1], kvg_d, c).then_inc(csem, 16); n += 16
                e.wait_ge(csem, n)

            @block.gpsimd
            def _(e):
                e.memset(identf, 1.0).then_inc(psem, 1)
                e.memset(maskc[:], 1.0).then_inc(psem, 1)
                e.wait_ge(psem, 2)
                e.affine_select(out=identf, in_=identf, pattern=[[-1, 128]],
                                compare_op=ALU.is_equal, fill=0.0, base=0, channel_multiplier=1).then_inc(psem, 1)
                e.affine_select(out=maskc[:], in_=maskc[:], pattern=[[-1, 128]],
                                compare_op=ALU.is_ge, fill=0.0, base=-1, channel_multiplier=1).then_inc(psem, 1)
                e.wait_ge(psem, 4)
                e.tensor_scalar(out=negm[:], in0=maskc[:], scalar1=-1.0, scalar2=30000.0, op0=ALU.add, op1=ALU.mult)
                e.memset(negm2[:], 0.0).then_inc(psem, 1)
                e.wait_ge(psem, 5)
                e.memset(negm2[64:128, 0:64], -30000.0)
                e.tensor_copy(out=ident[:], in_=identf)
                e.memset(ones_bf[:], 1.0)
                e.memset(ones_f[:], 1.0)
                e.memset(v_mla[:].rearrange("p a b c -> p (a b c)"), 1.0)
                e.memset(h[:].rearrange("p a d -> p (a d)"), 0.0)
                e.memset(qm_sb[:].rearrange("p a d -> p (a d)"), 0.0)
                e.memset(qm_nope[:].rearrange("p a d -> p (a d)"), 0.0)
                e.memset(qm_rope[:].rearrange("p a d -> p (a d)"), 0.0)

        bank_ctr = [0]

        held = set()

        def bank():
            for _ in range(16):
                b = bank_ctr[0] % 8
                bank_ctr[0] += 1
                if b not in held:
                    return b
            raise RuntimeError("all PSUM banks held")

        def bank_pair():
            while True:
                if bank_ctr[0] % 2:
                    bank_ctr[0] += 1
                b = bank_ctr[0] % 8
                bank_ctr[0] += 2
                if b not in held and (b + 1) not in held:
                    return b

        slot_ctr = [0]

        def wload(pieces):
            k = slot_ctr[0] % NSLOT
            slot_ctr[0] += 1
            st = slots[k]
            prev = []
            for (dfn, src) in pieces:
                dst = dfn(st)
                op = S.add("pool", (lambda e, dst=dst, src=src: e.dma_start(out=dst, in_=src)),
                           writes=[("slot", k)], dma=f"slot{k}", nodep=prev, cost=(1000.0, dst.size() * 4.0))
                prev.append(op)
            return st, ("slot", k)

        def v3(st, a, b):
            return st[:, 0:a * b].rearrange("p (a b) -> p a b", a=a)

        COLD = [False]

        def pe_cost(n):
            return (max(64.0, n / 1.2), 110.0) if COLD[0] else (max(40.0, n / 2.4 + 12), 60.0)

        def mm(out, lhsT, rhs, start, stop, reads, writes, skip=False):
            cst = pe_cost(out.free_size())
            if skip:
                S.add("pe", (lambda e: e.matmul(out, lhsT=lhsT, rhs=rhs, start=start, stop=stop, skip_group_check=True)), reads, writes, cost=cst)
            else:
                S.add("pe", (lambda e: e.matmul(out, lhsT=lhsT, rhs=rhs, start=start, stop=stop)), reads, writes, cost=cst)

        def tr(out, in_, idn, reads, writes):
            S.add("pe", (lambda e: e.transpose(out, in_, idn)), reads, writes, cost=(107.0, 45.0) if COLD[0] else (56.0, 45.0))

        def act(out, in_, func, reads, writes, **kw):
            cst = (224.0 + 0.7 * in_.free_size(), 0.0)
            if globals().get("SIM_FREE_LN") and (func == AF.Ln or kw.get("scale") == -1.0):
                cst = (1.0, 0.0)
            S.add("act", (lambda e: e.activation(out=out, in_=in_, func=func, **kw)), reads, writes, cost=cst)

        def dve(fn, reads, writes, cost=None):
            S.add("dve", fn, reads, writes, cost=cost)

        def psbf(b):
            return ps[:, b, :].bitcast(BF16)

        def stage_load(g):
            if g < 0:
                S.add("sp", (lambda e: e.dma_start(out=h[0:NMETA, 0, :], in_=meta_d[:, :])),
                      writes=[("h", 0)], dma="hl0")
                return
            for i in range(4):
                r0 = g * G + i * 128
                S.add("sp", (lambda e, i=i, r0=r0: e.dma_start(out=h[:, i, :], in_=x_d[r0:r0 + 128, :])),
                      writes=[("h", i)], dma=f"hl{i}", cost=(100.0, 524288.0))

        def stage_rope_tab(g):
            kp = 0 if g < 0 else NMETA + (g % GPS) * G
            n = NMETA if g < 0 else G
            S.add("sp", (lambda e: e.dma_start(out=cosT[:, 0:n], in_=rope_d[0, :, kp:kp + n])),
                  writes=[("cosT",)], dma="ropec")
            S.add("sp", (lambda e: e.dma_start(out=sinT[:, 0:n], in_=rope_d[1, :, kp:kp + n])),
                  writes=[("sinT",)], dma="ropes")

        def stage_norm(tiles, gi):
            for i, (t0, r) in enumerate(tiles):
                junk = uT[0:r, 2 * i:2 * i + 2, :].rearrange("p a b -> p (a b)")
                act(junk, h[0:r, i, :], AF.Square, [("h", i)], [("uT",), ("st", 8 + i)], accum_out=stat[0:r, 8 + i:9 + i])
            for i, (t0, r) in enumerate(tiles):
                act(stat[0:r, 12 + i:13 + i], stat[0:r, 8 + i:9 + i], AF.Sqrt, [("st", 8 + i)], [("st", 12 + i)], scale=1.0 / D, bias=EPS)
            for i, (t0, r) in enumerate(tiles):
                rs = stat[0:r, 12 + i:13 + i]
                dve((lambda e, rs=rs: e.reciprocal(out=rs, in_=rs)), [("st", 12 + i)], [("st", 12 + i)])
            for i, (t0, r) in enumerate(tiles):
                rs = stat[0:r, 12 + i:13 + i]
                xb_ = xn2[i % 2]
                act(xb_[0:r, :], h[0:r, i, :], AF.Copy, [("h", i), ("st", 12 + i)], [("xn", 0)], scale=rs)
                b = bank()
                pv = psbf(b)
                for kc in range(8):
                    tr(pv[:, kc * 128:kc * 128 + r], xb_[0:r, kc * 128:(kc + 1) * 128], ident[0:r, 0:r],
                       [("xn", 0)], [("ps", b)])
                src = pv.rearrange("p (c t) -> p c t", c=8)[:, :, 0:r]
                dst = uT[:, :, t0:t0 + r]
                gv = gpre[:, gi, :].unsqueeze(2).to_broadcast([128, 8, r])
                dve((lambda e, dst=dst, src=src, gv=gv: e.tensor_tensor(out=dst, in0=src, in1=gv, op=ALU.mult)),
                    [("ps", b)], [("uT",)])

        def stage_ffn(tiles, f, tok):
            wg, wu, wd = wg_d[f], wu_d[f], wd_d[f]
            load_gpost(0 if f == 0 else 2)
            for blk in range(NFF // 2):
                c0 = blk * 256
                sa, ka = wload([(lambda st: v3(st, 8, 256), wg[:, c0:c0 + 256].rearrange("(kc p) n -> p kc n", p=128))])
                sbv, kb = wload([(lambda st: v3(st, 8, 256), wu[:, c0:c0 + 256].rearrange("(kc p) n -> p kc n", p=128))])
                A = v3(sa, 8, 256)
                B = v3(sbv, 8, 256)
                for j in range(2):
                    ffc = 2 * blk + j
                    bgt = bank()
                    for kc in range(8):
                        mm(ps[:, bgt, 0:tok], A[:, kc, j * 128:(j + 1) * 128], uT[:, kc, 0:tok], kc == 0, kc == 7,
                           [ka, ("uT",)], [("ps", bgt)])
                    bu = bank()
                    for kc in range(8):
                        mm(ps[:, bu, 0:tok], B[:, kc, j * 128:(j + 1) * 128], uT[:, kc, 0:tok], kc == 0, kc == 7,
                           [kb, ("uT",)], [("ps", bu)])
                    sgi = ffc % 2
                    act(sg[sgi][:, 0:tok], ps[:, bgt, 0:tok], AF.Silu, [("ps", bgt)], [("a", sgi)])
                    dve((lambda e, o=actT[:, ffc, 0:tok], a=ps[:, bu, 0:tok], b2=sg[sgi][:, 0:tok]:
                         e.tensor_tensor(out=o, in0=a, in1=b2, op=ALU.mult)),
                        [("ps", bu), ("a", sgi)], [("actT", ffc)])
            bank_ctr[0] = 0
            for blk in range(NFF // 2):
                r0 = blk * 256
                sw, kw = wload([(lambda st: v3(st, 2, D), wd[r0:r0 + 256, :].rearrange("(j p) n -> p j n", p=128))])
                W = v3(sw, 2, D)
                for j in range(2):
                    ffc = 2 * blk + j
                    for i, (t0, r) in enumerate(tiles):
                        for nh in range(2):
                            b = 2 * i + nh
                            mm(ps[0:r, b, :], actT[:, ffc, t0:t0 + r], W[:, j, nh * 512:(nh + 1) * 512],
                               ffc == 0, ffc == NFF - 1, [kw, ("actT", ffc)], [("ps", b)])
            stage_post(tiles, 0 if f == 0 else 2, half=True)
            bank_ctr[0] = 2 * len(tiles)

        def load_gpost(gi):
            S.add("sp", (lambda e: e.dma_start(out=gpost1[:, :], in_=gains_post_d[0:1, gi * D:(gi + 1) * D].partition_broadcast(128))),
                  writes=[("gpost",)], dma="gp", cost=(100.0, 524288.0))

        def stage_post(tiles, gi, half):
            k = 4.0 if half else 1.0
            for i, (t0, r) in enumerate(tiles):
                fps = ps[0:r, 2 * i:2 * i + 2, :]
                junk = actT[0:r, 2 * i:2 * i + 2, :].rearrange("p a b -> p (a b)")
                act(junk, fps, AF.Square, [("ps", 2 * i), ("ps", 2 * i + 1)],
                    [("actT", 2 * i), ("actT", 2 * i + 1), ("st", 16 + i)], accum_out=stat[0:r, 16 + i:17 + i])
            for i, (t0, r) in enumerate(tiles):
                act(stat[0:r, 20 + i:21 + i], stat[0:r, 16 + i:17 + i], AF.Sqrt, [("st", 16 + i)], [("st", 20 + i)],
                    scale=k / D, bias=k * EPS)
            for i, (t0, r) in enumerate(tiles):
                rs = stat[0:r, 20 + i:21 + i]
                dve((lambda e, rs=rs: e.reciprocal(out=rs, in_=rs)), [("st", 20 + i)], [("st", 20 + i)])
            for i, (t0, r) in enumerate(tiles):
                fps = ps[0:r, 2 * i:2 * i + 2, :]
                rs = stat[0:r, 20 + i:21 + i]
                pt, pk = (ptmp, [("E", 0), ("E", 1)]) if i % 2 == 0 else (ptmp2, [("SP", 0), ("SP", 1)])
                dve((lambda e, fps=fps, rs=rs, r=r, pt=pt: e.scalar_tensor_tensor(
                    out=pt[0:r, :], in0=fps, scalar=rs, in1=gpost1[0:r, :], op0=ALU.mult, op1=ALU.mult)),
                    [("ps", 2 * i), ("ps", 2 * i + 1), ("st", 20 + i), ("gpost",)], pk, cost=(1300.0, 0.0))
                dve((lambda e, i=i, r=r, pt=pt: e.tensor_tensor(out=h[0:r, i, :], in0=h[0:r, i, :], in1=pt[0:r, :], op=ALU.add)),
                    pk + [("h", i)], [("h", i)], cost=(1000.0, 0.0))

        def win_cols(c0, n):
            return win_d[:, c0:c0 + n].rearrange("(kc p) n -> p kc n", p=128)

        def stage_proj(tiles, tok, g):
            is_meta = g < 0
            gi = "m" if is_meta else g % GPS
            kp = 0 if is_meta else NMETA + (g % GPS) * G
            U = [("uT",)]
            PP = 99 if (dbg is None or is_meta) else dbg.get("pp", 99)
            for which in ([1] if is_meta else [0, 1]):
                for half in range(2):
                    c0 = which * 512 + half * 256
                    st, kk = wload([(lambda s_: v3(s_, 8, 256), win_cols(c0, 256))])
                    W = v3(st, 8, 256)
                    for j in range(2):
                        c = half * 2 + j
                        b = bank()
                        for kc in range(8):
                            mm(ps[:, b, 0:tok], W[:, kc, j * 128:(j + 1) * 128], uT[:, kc, 0:tok], kc == 0, kc == 7,
                               [kk] + U, [("ps", b)])
                        if which == 0:
                            act(qm_sb[0:64, 2 * c, 0:tok], ps[0:64, b, 0:tok], AF.Copy, [("ps", b)], [("qT_sb",)], scale=0.125)
                            act(qm_sb[64:128, 2 * c + 1, 0:tok], ps[64:128, b, 0:tok], AF.Copy, [("ps", b)], [("qT_sb",)], scale=0.125)
                        else:
                            act(kT_sb[:, c, kp:kp + tok], ps[:, b, 0:tok], AF.Copy, [("ps", b)], [("kT_sb", gi)])
            if PP <= 1:
                return
            sv = []
            for half in range(2):
                st, kk = wload([(lambda s_: v3(s_, 8, 256), win_cols(1024 + half * 256, 256))])
                sv.append((v3(st, 8, 256), kk))
            for i, (t0, r) in enumerate(tiles):
                b = bank()
                for half in range(2):
                    W, kk = sv[half]
                    for kc in range(8):
                        mm(ps[0:r, b, half * 256:(half + 1) * 256], uT[:, kc, t0:t0 + r], W[:, kc, :], kc == 0, kc == 7,
                           [kk] + U, [("ps", b)])
                vb = 0 if is_meta else 1 + (g % GPS) * 4 + i
                dve((lambda e, o=v_sb[0:r, vb, :], a=ps[0:r, b, :]: e.tensor_copy(out=o, in_=a)),
                    [("ps", b)], [("v_sb", gi)])
            if PP <= 2:
                return
            if not is_meta:
                st1, k1 = wload([(lambda s_: v3(s_, 8, 256), win_cols(1536, 256))])
                st2, k2 = wload([(lambda s_: v3(s_, 8, 256), win_cols(1792, 256))])
                lw = [(v3(st1, 8, 256), 0, k1), (v3(st1, 8, 256), 128, k1), (v3(st2, 8, 256), 0, k2)]
                for c in range(3):
                    W, off, kk = lw[c]
                    b = bank()
                    for kc in range(8):
                        mm(ps[:, b, 0:tok], W[:, kc, off:off + 128], uT[:, kc, 0:tok], kc == 0, kc == 7, [kk] + U, [("ps", b)])
                    dve((lambda e, o=cqT[:, c, 0:tok], a=ps[:, b, 0:tok], s=qg[:, c:c + 1]:
                         e.tensor_scalar(out=o, in0=a, scalar1=s, scalar2=None, op0=ALU.mult)),
                        [("ps", b)], K_cq + [("ser",)])
                    act(pT[c][:, 0:tok], ps[:, b, 0:tok], AF.Square, [("ps", b)], [("pT", c), ("ser",)])
                if PP == 3 and dbg.get("sub", 0) == 1:
                    return
                b = bank()
                for c in range(3):
                    mm(ps[:, b, 0:tok], ones_bf[:, :], pT[c][:, 0:tok], c == 0, c == 2, [("pT", c)], [("ps", b)])
                act(rq_rep[:, 0:tok], ps[:, b, 0:tok], AF.Sqrt, [("ps", b)], [("E", 0)], scale=1.0 / 384, bias=EPS)
                dve((lambda e: e.reciprocal(out=rq_rep[:, 0:tok], in_=rq_rep[:, 0:tok])), [("E", 0)], [("E", 0)])
            if PP <= 3:
                return
            st1, k1 = wload([(lambda s_: v3(s_, 8, 256), win_cols(1920, 256))])
            W = v3(st1, 8, 256)
            for c in range(2):
                b = bank()
                for kc in range(8):
                    mm(ps[:, b, 0:tok], W[:, kc, c * 128:(c + 1) * 128], uT[:, kc, 0:tok], kc == 0, kc == 7, [k1] + U, [("ps", b)])
                dve((lambda e, o=ckvT[:, c, 0:tok], a=ps[:, b, 0:tok], s=kvg[:, c:c + 1]:
                     e.tensor_scalar(out=o, in0=a, scalar1=s, scalar2=None, op0=ALU.mult)),
                    [("ps", b)], K_ckv + [("ser",)])
                act(pT[c][:, 0:tok], ps[:, b, 0:tok], AF.Square, [("ps", b)], [("pT", c), ("ser",)])
            b = bank()
            for c in range(2):
                mm(ps[:, b, 0:tok], ones_bf[:, :], pT[c][:, 0:tok], c == 0, c == 1, [("pT", c)], [("ps", b)])
            act(rkv_rep[:, 0:tok], ps[:, b, 0:tok], AF.Sqrt, [("ps", b)], [("E", 1)], scale=1.0 / 256, bias=EPS)
            dve((lambda e: e.reciprocal(out=rkv_rep[:, 0:tok], in_=rkv_rep[:, 0:tok])), [("E", 1)], [("E", 1)])
            for i, (t0, r) in enumerate(tiles):
                b = bank()
                for c in range(2):
                    mm(ps[0:r, b, 0:1], pT[c][:, t0:t0 + r], ones_bf[:, 0:1], c == 0, c == 1, [("pT", c)], [("ps", b)])
                rs = stat[0:r, 4 + i:5 + i]
                act(rs, ps[0:r, b, 0:1], AF.Sqrt, [("ps", b)], [("st", 4 + i)], scale=1.0 / 256, bias=EPS)
                dve((lambda e, rs=rs: e.reciprocal(out=rs, in_=rs)), [("st", 4 + i)], [("st", 4 + i)])
            if PP <= 5:
                return
            pieces = []
            for rep in range(3):
                pieces.append((lambda s_, rep=rep: v3(s_, 8, 192)[:, :, rep * 32:(rep + 1) * 32], win_cols(2176, 32)))
                pieces.append((lambda s_, rep=rep: v3(s_, 8, 192)[:, :, 96 + rep * 32:96 + rep * 32 + 16], win_cols(2192, 16)))
                pieces.append((lambda s_, rep=rep: v3(s_, 8, 192)[:, :, 96 + rep * 32 + 16:96 + rep * 32 + 32], win_cols(2176, 16)))
            st, kk = wload(pieces)
            W = v3(st, 8, 192)
            braw = bank()
            for kc in range(8):
                mm(ps[0:96, braw, 0:tok], W[:, kc, 0:96], uT[:, kc, 0:tok], kc == 0, kc == 7, [kk] + U, [("ps", braw)])
            brot = bank()
            for kc in range(8):
                mm(ps[0:96, brot, 0:tok], W[:, kc, 96:192], uT[:, kc, 0:tok], kc == 0, kc == 7, [kk] + U, [("ps", brot)])
            dve((lambda e, braw=braw: e.tensor_tensor(out=rt1[:, 0:tok], in0=ps[0:96, braw, 0:tok], in1=cosT[:, 0:tok], op=ALU.mult)),
                [("ps", braw), ("cosT",)], [("Sfx", 0)])
            dve((lambda e, brot=brot: e.tensor_tensor(out=rt2[:, 0:tok], in0=ps[0:96, brot, 0:tok], in1=sinT[:, 0:tok], op=ALU.mult)),
                [("ps", brot), ("sinT",)], [("Sfx", 1)])
            dve((lambda e: e.tensor_tensor(out=kT_rope[:, kp:kp + tok], in0=rt1[:, 0:tok], in1=rt2[:, 0:tok], op=ALU.add)),
                [("Sfx", 0), ("Sfx", 1)], [("kT_rope", gi)])
            if PP <= 6:
                return
            if not is_meta:
                stn, kn = wload([(lambda s_, kc=kc: s_[:, kc * 512:(kc + 1) * 512].rearrange("p (h d) -> p h d", h=8),
                                  wuq_d[kc * 128:(kc + 1) * 128, :, 0:64]) for kc in range(3)])
                Wn = stn[:, 0:1536].rearrange("p (kc hd) -> p kc hd", kc=3)
                for c in range(4):
                    b = bank()
                    for kc in range(3):
                        mm(ps[:, b, 0:tok], Wn[:, kc, c * 128:(c + 1) * 128], cqT[:, kc, 0:tok], kc == 0, kc == 2,
                           [kn] + K_cq, [("ps", b)])
                    dve((lambda e, o=qm_nope[0:64, 2 * c, 0:tok], a=ps[0:64, b, 0:tok]:
                         e.tensor_tensor(out=o, in0=a, in1=rq_rep[0:64, 0:tok], op=ALU.mult)),
                        [("ps", b), ("E", 0)], [("qT_nope",)])
                    dve((lambda e, o=qm_nope[64:128, 2 * c + 1, 0:tok], a=ps[64:128, b, 0:tok]:
                         e.tensor_tensor(out=o, in0=a, in1=rq_rep[64:128, 0:tok], op=ALU.mult)),
                        [("ps", b), ("E", 0)], [("qT_nope",)])
                if PP <= 7:
                    return
                def rv(s_, base, kc):
                    return s_[:, base + kc * 256:base + (kc + 1) * 256].rearrange("p (h d) -> p h d", h=8)
                rp = []
                for kc in range(3):
                    rws = slice(kc * 128, (kc + 1) * 128)
                    rp.append((lambda s_, kc=kc: rv(s_, 0, kc), wuq_d[rws, :, 64:96]))
                    rp.append((lambda s_, kc=kc: rv(s_, 768, kc)[:, :, 0:16], wuq_d[rws, :, 80:96]))
                    rp.append((lambda s_, kc=kc: rv(s_, 768, kc)[:, :, 16:32], wuq_d[rws, :, 64:80]))
                str_, kr = wload(rp)
                Wr = str_[:, 0:768].rearrange("p (kc hd) -> p kc hd", kc=3)
                Wt = str_[:, 768:1536].rearrange("p (kc hd) -> p kc hd", kc=3)
                for gq in range(3):
                    h0 = gq * 3
                    nh_ = 3 if gq < 2 else 2
                    m = nh_ * 32
                    braw = bank()
                    for kc in range(3):
                        mm(ps[0:m, braw, 0:tok], Wr[:, kc, h0 * 32:h0 * 32 + m], cqT[:, kc, 0:tok], kc == 0, kc == 2,
                           [kr] + K_cq, [("ps", braw)])
                    brot = bank()
                    for kc in range(3):
                        mm(ps[0:m, brot, 0:tok], Wt[:, kc, h0 * 32:h0 * 32 + m], cqT[:, kc, 0:tok], kc == 0, kc == 2,
                           [kr] + K_cq, [("ps", brot)])
                    dve((lambda e, m=m, braw=braw: e.tensor_tensor(out=rt1[0:m, 0:tok], in0=ps[0:m, braw, 0:tok], in1=cosT[0:m, 0:tok], op=ALU.mult)),
                        [("ps", braw), ("cosT",)], [("Sfx", 0)])
                    dve((lambda e, m=m, brot=brot: e.tensor_tensor(out=rt2[0:m, 0:tok], in0=ps[0:m, brot, 0:tok], in1=sinT[0:m, 0:tok], op=ALU.mult)),
                        [("ps", brot), ("sinT",)], [("Sfx", 1)])
                    dve((lambda e, m=m: e.tensor_tensor(out=rt1[0:m, 0:tok], in0=rt1[0:m, 0:tok], in1=rt2[0:m, 0:tok], op=ALU.add)),
                        [("Sfx", 0), ("Sfx", 1)], [("Sfx", 0)])
                    for jh in range(nh_):
                        rr = slice(32 * jh, 32 * jh + 32)
                        dve((lambda e, rr=rr, hq=h0 + jh: e.tensor_tensor(out=qm_rope[rr, hq, 0:tok], in0=rt1[rr, 0:tok], in1=rq_rep[rr, 0:tok], op=ALU.mult)),
                            [("Sfx", 0), ("E", 0)], [("qT_rope",)])
            if PP <= 8:
                return
            stk, kkk = wload([(lambda s_, kc=kc: s_[:, kc * 512:(kc + 1) * 512].rearrange("p (h d) -> p h d", h=8),
                               wukv_d[kc * 128:(kc + 1) * 128, :, 0:64]) for kc in range(2)])
            Wk = stk[:, 0:1024].rearrange("p (kc hd) -> p kc hd", kc=2)
            for c in range(4):
                b = bank()
                for kc in range(2):
                    mm(ps[:, b, 0:tok], Wk[:, kc, c * 128:(c + 1) * 128], ckvT[:, kc, 0:tok], kc == 0, kc == 1,
                       [kkk] + K_ckv, [("ps", b)])
                dve((lambda e, o=kT_nope[:, c, kp:kp + tok], a=ps[:, b, 0:tok]:
                     e.tensor_tensor(out=o, in0=a, in1=rkv_rep[:, 0:tok], op=ALU.mult)),
                    [("ps", b), ("E", 1)], [("kT_nope", gi)])
            stv, kv_ = wload([(lambda s_, kc=kc: s_[:, kc * 512:(kc + 1) * 512].rearrange("p (h d) -> p h d", h=8),
                               wukv_d[kc * 128:(kc + 1) * 128, :, 64:128]) for kc in range(2)])
            Wv = stv[:, 0:1024].rearrange("p (kc hd) -> p kc hd", kc=2)
            for i, (t0, r) in enumerate(tiles):
                b = bank()
                for kc in range(2):
                    mm(ps[0:r, b, :], ckvT[:, kc, t0:t0 + r], Wv[:, kc, :], kc == 0, kc == 1, [kv_] + K_ckv, [("ps", b)])
                vb = 0 if is_meta else 1 + (g % GPS) * 4 + i
                act(v_mla[0:r, vb, :, 0:64], ps[0:r, b, :].rearrange("p (h d) -> p h d", h=8), AF.Copy,
                    [("ps", b), ("st", 4 + i)], [("v_mla", gi)], scale=stat[0:r, 4 + i:5 + i])

        def key_blocks(j):
            bl = [(0, NMETA, 0)]
            for m in range(j + 1):
                bl.append((NMETA + 128 * m, 128, 1 + m))
            return bl

        rotE = [0]
        rotS = [0]
        rotA = [0]
        rotT = [0]
        rotP = [0]
        rotQ = [0]
        NSTG = max(len(CFG["sb_stages"]), len(CFG["mla_stages"]))

        def run_pipeline(units):
            n = len(units)
            for k in range(n + NSTG - 1):
                if CFG["order"] == "rev":
                    sorder = list(range(NSTG - 1, -1, -1))
                elif CFG["order"] == "fwd":
                    sorder = list(range(NSTG))
                else:
                    sorder = [0] + list(range(NSTG - 1, 0, -1))
                for sidx in sorder:
                    u = k - sidx
                    if 0 <= u < n and units[u][sidx] is not None:
                        units[u][sidx]()

        def sb_unit(ctx, hd, ch, is_last_chunk, first_flag, last_flag, kres, vres, fin):
            t0 = ctx["t0"]
            c = hd // 2
            pb = 64 * (hd % 2)
            k0 = ch[0][0]
            kn = sum(w for (_, w, _) in ch)
            nb = len(ch)
            st = {}

            def sA():
                rE = rotE[0] % NE
                rotE[0] += 1
                rQ = rotQ[0] % NSP
                rotQ[0] += 1
                st["rE"] = rE
                st["rQ"] = rQ
                E = Ebuf[rE]
                SPv = SPbuf[rQ]
                if "Z" not in [n_ for g_ in CFG["sb_stages"] for n_ in g_]:
                    sZ()
                zb = st["zb"]
                act(E[:, 0:kn], ps[:, zb, 0:kn], AF.Exp, [("ps", zb)], [("E", rE)])
                held.discard(zb)
                act(SPv[:, 0:kn], E[:, 0:kn], AF.Ln, [("E", rE)], [("SP", rQ)], bias=1.0)

            def sZ():
                if ctx["ob"] is None:
                    ctx["ob"] = bank()
                    held.add(ctx["ob"])
                zb = bank()
                held.add(zb)
                st["zb"] = zb
                mm(ps[:, zb, 0:kn], qm_sb[:, hd, t0:t0 + 128], kT_sb[:, c, k0:k0 + kn], True, not is_last_chunk,
                   [("qT_sb",)] + kres, [("ps", zb)])
                if is_last_chunk:
                    mm(ps[:, zb, kn - 128:kn], ident[:, :], negm[:, :], False, True, [], [("ps", zb)])

            def sA2():
                rQ = st["rQ"]
                SPv = SPbuf[rQ]
                rS = rotS[0] % NS
                rotS[0] += 1
                st["rS"] = rS
                carry = ctx["carry"].get(hd)
                init = 0.0 if carry is None else carry[0]
                rd = [("SP", rQ)] + ([] if carry is None else [carry[1]])
                dve((lambda e: e.tensor_tensor_scan(
                    out=Sfx[rS][:, 0:kn][:, ::-1], data0=ones_f[:, 0:kn], data1=SPv[:, 0:kn][:, ::-1],
                    initial=init, op0=ALU.mult, op1=ALU.add)), rd, [("Sfx", rS)], cost=(2.25 * kn, 0.0))
                ctx["carry"][hd] = (Sfx[rS][:, 0:1], ("Sfx", rS))

            def sB():
                rE, rQ, rS = st["rE"], st["rQ"], st["rS"]
                E = Ebuf[rE]
                SPv = SPbuf[rQ]
                rA = rotA[0] % NA
                rotA[0] += 1
                st["rA"] = rA
                A = abuf[rA]
                act(SPv[:, 0:kn], Sfx[rS][:, 0:kn], AF.Exp, [("Sfx", rS)], [("SP", rQ)], scale=-1.0)
                S.add("pool", (lambda e: e.tensor_tensor(out=A[:, 0:kn], in0=E[:, 0:kn], in1=SPv[:, 0:kn], op=ALU.mult)),
                      [("E", rE), ("SP", rQ)], [("a", rA)], cost=(150.0 + 1.72 * kn, 0.0))

            def sC():
                rA = st["rA"]
                A = abuf[rA]
                tb = bank()
                held.add(tb)
                st["tb"] = tb
                pv = psbf(tb)
                off = 0
                for bi, (kcol, w, vb) in enumerate(ch):
                    tr(pv[0:w, bi * 128:(bi + 1) * 128], A[:, off:off + w], ident[:, :], [("a", rA)], [("ps", tb)])
                    off += w

            def sD():
                tb = st["tb"]
                rT = rotT[0] % NT_
                rotT[0] += 1
                st["rT"] = rT
                pv = psbf(tb)
                b0 = 0
                if ch[0][1] < 128:
                    w0 = ch[0][1]
                    dve((lambda e: e.tensor_copy(out=aT[rT][0:w0, 0, :], in_=pv[0:w0, 0:128])),
                        [("ps", tb)], [("aT", rT)], cost=(120.0, 0.0))
                    b0 = 1
                if nb > b0:
                    dve((lambda e: e.tensor_copy(out=aT[rT][:, b0:nb, :].rearrange("p a b -> p (a b)"), in_=pv[:, b0 * 128:nb * 128])),
                        [("ps", tb)], [("aT", rT)], cost=(60.0 + 0.48 * (nb - b0) * 128, 0.0))
                held.discard(tb)

            def sE():
                rT = st["rT"]
                ob = ctx["ob"]
                for bi, (kcol, w, vb) in enumerate(ch):
                    mm(ps[:, ob, hd * 64:(hd + 1) * 64], aT[rT][0:w, bi, :], v_sb[0:w, vb, hd * 64:(hd + 1) * 64],
                       first_flag and bi == 0, last_flag and bi == nb - 1, [("aT", rT)] + vres, [("ps", ob)])
                if fin:
                    dve((lambda e: e.tensor_copy(out=ytok[:, :], in_=ps[:, ob, :])), [("ps", ob)], [("ytok",)])
                    held.discard(ob)
                    tb = bank()
                    pv = psbf(tb)
                    for cc in range(4):
                        tr(pv[:, cc * 128:(cc + 1) * 128], ytok[:, cc * 128:(cc + 1) * 128], ident[:, :], [("ytok",)], [("ps", tb)])
                    act(actT[:, 8:12, t0:t0 + 128], pv[:, 0:512].rearrange("p (c t) -> p c t", c=4), AF.Copy, [("ps", tb)], K_ysb)

            prim = {"Z": sZ, "A": sA, "A2": sA2, "B": sB, "C": sC, "D": sD, "E": sE}

            def mk(names):
                def f():
                    for n_ in names:
                        Sched.curtag = f"sb{ctx['t0'] // 128}h{hd}k{k0}:{n_}"
                        prim[n_]()
                return f
            stg = [mk(g_) for g_ in CFG["sb_stages"]]
            return stg + [None] * (NSTG - len(stg))

        def mla_unit(cm, hd, blk, ib, is_first_blk, knres, krres, vres, fin, last_of_group):
            kcol, w, vb = blk
            c = hd // 2
            pb = 64 * (hd % 2)
            gq = hd // 3
            rb = 32 * (hd % 3)
            hl = hd % 2
            q4 = hd // 2
            t_lo = 0 if ib is None else ib
            N = (4 - t_lo) * 128
            tc0 = t_lo * 128
            st = {}

            def sA():
                if cm["obm"] is None:
                    cm["obm"] = [bank(), bank()]
                    for b_ in cm["obm"]:
                        held.add(b_)
                zb = bank()
                held.add(zb)
                st["zb"] = zb
                mm(ps[0:w, zb, 0:N], kT_nope[:, c, kcol:kcol + w], qm_nope[:, hd, tc0:G], True, False,
                   [("qT_nope",)] + knres, [("ps", zb)])
                mm(ps[0:w, zb, 0:N], kT_rope[:, kcol:kcol + w], qm_rope[:, hd, tc0:G], False, ib is None,
                   [("qT_rope",)] + krres, [("ps", zb)])
                if ib is not None:
                    mm(ps[:, zb, 0:128], ident[:, :], negm2[:, :], False, True, [], [("ps", zb)])

            def sB():
                zb = st["zb"]
                r = rotP[0] % NP
                rotP[0] += 1
                st["r"] = r
                P = pT[r]
                act(P[0:w, 0:N], ps[0:w, zb, 0:N], AF.Exp, [("ps", zb)], [("pT", r)], scale=MLA_SCALE)
                held.discard(zb)

            def sC():
                r = st["r"]
                P = pT[r]
                obm = cm["obm"]
                for i in range(t_lo, 4):
                    off = (i - t_lo) * 128
                    ob = obm[i // 2]
                    oc = (i % 2) * 130 + hl * 65
                    mm(ps[:, ob, oc:oc + 65], P[0:w, off:off + 128], v_mla[0:w, vb, hd, :],
                       is_first_blk and i % 2 == 0, (ib is not None and i == ib), [("pT", r)] + vres, [("ps", ob)], skip=True)
                if fin:
                    for i in range(4):
                        ob = obm[i // 2]
                        base = (i % 2) * 130
                        o3 = ps[:, ob, base:base + 130].rearrange("p (h d) -> p h d", h=2)
                        dve((lambda e, o3=o3, i=i: e.reciprocal(out=rc[:, 2 * i:2 * i + 2].unsqueeze(2), in_=o3[:, :, 64:65])),
                            [("ps", ob)], [("rc", i)])
                        dve((lambda e, o3=o3, i=i: e.tensor_tensor(
                            out=ytok2[:, i * 128:(i + 1) * 128].rearrange("p (h d) -> p h d", h=2), in0=o3[:, :, 0:64],
                            in1=rc[:, 2 * i:2 * i + 2].unsqueeze(2).to_broadcast([128, 2, 64]), op=ALU.mult)),
                            [("ps", ob), ("rc", i)], [("ytok2", i)])
                    if last_of_group:
                        for b_ in obm:
                            held.discard(b_)
                        cm["obm"] = None
                    tb = bank()
                    pv = psbf(tb)
                    for i in range(4):
                        tr(pv[:, i * 128:(i + 1) * 128], ytok2[:, i * 128:(i + 1) * 128], ident[:, :], [("ytok2", i)], [("ps", tb)])
                    act(actT[:, 12 + q4, :], pv[:, 0:512], AF.Copy, [("ps", tb)], [("actT", 12 + q4)])

            prim = {"A": sA, "B": sB, "C": sC}

            def mk(names):
                def f():
                    for n_ in names:
                        Sched.curtag = f"mla_h{hd}b{vb}:{n_}"
                        prim[n_]()
                return f
            stg = [mk(g_) for g_ in CFG["mla_stages"]]
            return stg + [None] * (NSTG - len(stg))

        def stage_attn(g):
            COLD[0] = False
            try:
                _stage_attn(g)
            finally:
                COLD[0] = False

        def _stage_attn(g):
            gs_ = g % GPS
            gl = ["m"] + list(range(gs_ + 1))
            kres = [("kT_sb", q) for q in gl]
            vres = [("v_sb", q) for q in gl]
            knres = [("kT_nope", q) for q in gl]
            krres = [("kT_rope", q) for q in gl]
            vmres = [("v_mla", q) for q in gl]
            su = []
            for i in range(4):
                j = gs_ * 4 + i
                blocks = key_blocks(j)
                chunks = [blocks[0:4]]
                rest = blocks[4:]
                while rest:
                    chunks.append(rest[0:4])
                    rest = rest[4:]
                ctx = {"t0": i * 128, "ob": None, "carry": {}}
                nch = len(chunks)
                for hd in range(8):
                    for q_, ci in enumerate(range(nch - 1, -1, -1)):
                        su.append(sb_unit(ctx, hd, chunks[ci], ci == nch - 1, q_ == 0, q_ == nch - 1, kres, vres,
                                          hd == 7 and q_ == nch - 1))
            allb = key_blocks(gs_ * 4 + 3)
            mu = []
            cm = {"obm": None}
            for hd in range(8):
                for bi_, blk in enumerate(allb):
                    xb = blk[2] - 1
                    ib = xb - gs_ * 4 if xb >= gs_ * 4 else None
                    lastb = bi_ == len(allb) - 1
                    mu.append(mla_unit(cm, hd, blk, ib, bi_ == 0, knres, krres, vmres,
                                       hd % 2 == 1 and lastb, hd == 7 and lastb))
            units = []
            a = b = 0
            while a < len(su) or b < len(mu):
                if b >= len(mu) or (a < len(su) and a * len(mu) <= b * len(su)):
                    units.append(su[a])
                    a += 1
                else:
                    units.append(mu[b])
                    b += 1
            run_pipeline(units)

        def stage_mixout(tiles, tok):
            load_gpost(1)
            for cp in range(4):
                c0 = cp * 256
                so, ko = wload([
                    (lambda s_: v3(s_, 8, 256)[:, 0:4, :], wsbo_d[:, c0:c0 + 256].rearrange("(kc p) n -> p kc n", p=128)),
                    (lambda s_: v3(s_, 8, 256)[:, 4:8, :], wmlao_d[:, c0:c0 + 256].rearrange("(kc p) n -> p kc n", p=128)),
                ])
                WO = v3(so, 8, 256)
                s1, k1 = wload([(lambda s_: v3(s_, 8, 256), win_cols(2208 + c0, 256))])
                s2, k2 = wload([(lambda s_: v3(s_, 8, 256), win_cols(2208 + 1024 + c0, 256))])
                W1 = v3(s1, 8, 256)
                W2 = v3(s2, 8, 256)
                for jj in range(2):
                    c = 2 * cp + jj
                    cs = slice(jj * 128, (jj + 1) * 128)
                    b1 = bank()
                    for kc in range(4):
                        mm(ps[:, b1, 0:tok], WO[:, kc, cs], yT_sb[:, kc, 0:tok], kc == 0, kc == 3, [ko] + K_ysb, [("ps", b1)])
                    b2 = bank()
                    for kc in range(4):
                        mm(ps[:, b2, 0:tok], WO[:, 4 + kc, cs], yT_mla[:, kc, 0:tok], kc == 0, kc == 3, [ko] + K_ymla, [("ps", b2)])
                    b3 = bank()
                    for kc in range(8):
                        mm(ps[:, b3, 0:tok], W1[:, kc, cs], uT[:, kc, 0:tok], kc == 0, kc == 7, [k1, ("uT",)], [("ps", b3)])
                    b4 = bank()
                    for kc in range(8):
                        mm(ps[:, b4, 0:tok], W2[:, kc, cs], uT[:, kc, 0:tok], kc == 0, kc == 7, [k2, ("uT",)], [("ps", b4)])
                    act(gs[:, 0:tok], ps[:, b3, 0:tok], AF.Sigmoid, [("ps", b3)], [("pT", 0)], bias=bg[:, c:c + 1])
                    act(gm[:, 0:tok], ps[:, b4, 0:tok], AF.Sigmoid, [("ps", b4)], [("pT", 1)], bias=bg[:, 8 + c:9 + c])
                    dve((lambda e, b1=b1: e.tensor_tensor(out=mt1[:, 0:tok], in0=ps[:, b1, 0:tok], in1=gs[:, 0:tok], op=ALU.mult)),
                        [("ps", b1), ("pT", 0)], [("E", 0)])
                    dve((lambda e, b2=b2: e.tensor_tensor(out=mt2[:, 0:tok], in0=ps[:, b2, 0:tok], in1=gm[:, 0:tok], op=ALU.mult)),
                        [("ps", b2), ("pT", 1)], [("E", 1)])
                    dve((lambda e, c=c: e.tensor_tensor(out=mixT[:, c, 0:tok], in0=mt1[:, 0:tok], in1=mt2[:, 0:tok], op=ALU.add)),
                        [("E", 0), ("E", 1)], [("actT", c)])
            ws = []
            for q in range(4):
                st, kk = wload([(lambda s_: v3(s_, 2, D), wout_d[q * 256:(q + 1) * 256, :].rearrange("(j p) n -> p j n", p=128))])
                ws.append((v3(st, 2, D), kk))
            bank_ctr[0] = 0
            for i, (t0, r) in enumerate(tiles):
                for nh in range(2):
                    b = 2 * i + nh
                    for kc in range(8):
                        W, kk = ws[kc // 2]
                        mm(ps[0:r, b, :], mixT[:, kc, t0:t0 + r], W[:, kc % 2, nh * 512:(nh + 1) * 512], kc == 0, kc == 7,
                           [kk, ("actT", kc)], [("ps", b)])
            stage_post(tiles, 1, half=False)
            bank_ctr[0] = 2 * len(tiles)

        def stage_store(g):
            ops_ = []
            for i in range(4):
                r0 = g * G + i * 128
                ops_.append(S.add("sp", (lambda e, i=i, r0=r0: e.dma_start(out=out_d[r0:r0 + 128, :], in_=h[:, i, :])),
                                  reads=[("h", i)], writes=[("out", g, i)], dma=f"ho{i}", cost=(100.0, 524288.0)))
            return ops_

        MT = [(0, NMETA)]
        GT = [(i * 128, 128) for i in range(4)]
        prog = []
        prog += [("load", -1), ("rope", -1), ("norm", MT, 0), ("ffn", MT, 0, NMETA), ("norm", MT, 1), ("proj", MT, NMETA, -1)]
        for g in range(NG):
            prog += [("load", g), ("rope", g), ("norm", GT, 0), ("ffn", GT, 0, G), ("norm", GT, 1), ("proj", GT, G, g),
                     ("attn", g), ("mixout", GT, G), ("norm", GT, 2), ("ffn", GT, 1, G), ("store", g)]
        if dbg is not None and "simattn" in dbg:
            COLD[0] = False
            stage_attn(dbg["simattn"])
            res = S.simulate()
            if dbg.get("crit"):
                allops = [o for e_ in S.ENGS for o in S.ops[e_]]
                last = max(allops, key=lambda o: S.sim_done[id(o)])
                import collections
                cat = collections.Counter()
                op = last
                while op is not None:
                    kind, p = S.sim_pred[id(op)]
                    cat[(op.eng, kind)] += 1
                    dur = S.sim_done[id(op)] - S.sim_start[id(op)]
                    cat[("time", op.eng)] += dur
                    op = p
                print(sorted(cat.items(), key=lambda kv: -kv[1])[:14])
                op = last
                path = []
                while op is not None and len(path) < 400:
                    kind, p = S.sim_pred[id(op)]
                    path.append(f"{S.sim_start[id(op)]/1e3:8.2f} {op.eng:4s} {op.tag:22s} {kind} dur={S.sim_done[id(op)]-S.sim_start[id(op)]:.0f}")
                    op = p
                for l_ in path[200:260][::-1]:
                    print(l_)
            return res
        if dbg is not None:
            prog = prog[:dbg["nstage"]]
        fns = {"load": stage_load, "rope": stage_rope_tab, "norm": stage_norm, "ffn": stage_ffn, "proj": stage_proj,
               "attn": stage_attn, "mixout": stage_mixout, "store": stage_store}
        stored = []
        for st_ in prog:
            fns[st_[0]](*st_[1:])
            if st_[0] == "store":
                stored.append(st_[1])
        fin = S.add("sp", None, reads=[("out", g, i) for g in stored for i in range(4)])
        if dbg is not None and dbg.get("simall"):
            res = S.simulate()
            return res[0], res[1], S.dma_busy
        S.finalize()

        esem = {e_: es.enter_context(nc.semaphore(f"es_{e_}")) for e_ in Sched.ENGS}
        dsem = {n_: es.enter_context(nc.semaphore(f"ds_{n_}")) for n_ in S.dmacnt}
        with nc.Block() as block:
            @block.tensor
            def _(e):
                S.run(e, "pe", esem, dsem)

            @block.scalar
            def _(e):
                S.run(e, "act", esem, dsem)

            @block.vector
            def _(e):
                S.run(e, "dve", esem, dsem)

            @block.gpsimd
            def _(e):
                S.run(e, "pool", esem, dsem)

            @block.sync
            def _(e):
                S.run(e, "sp", esem, dsem)
        if dbg is not None and dbg.get("dump"):
            T = dict(h=h, uT=uT, actT=actT, kT_sb=kT_sb, v_sb=v_sb, kT_nope=kT_nope, kT_rope=kT_rope, v_mla=v_mla,
                     qT_sb=qm_sb, qT_nope=qm_nope, qT_rope=qm_rope, xn=xn, stat=stat,
                     gpre=gpre, bg=bg, cosT=cosT, sinT=sinT, EB=EB, ytok=ytok)
            dsm = es.enter_context(nc.semaphore("dumpsem"))
            with nc.Block() as block:
                @block.gpsimd
                def _(e):
                    n = 0
                    for name in dbg["dump"]:
                        t = T[name]
                        shp = list(t.shape)
                        dd = nc.dram_tensor("dbg_" + name, shp, F32, kind="ExternalOutput").ap()
                        e.dma_start(out=dd, in_=t[:]).then_inc(dsm, 16)
                        n += 16
                    e.wait_ge(dsm, n)
    return nc


def _rope_table():
    half = 16
    inv = (10000.0 ** (-np.arange(half, dtype=np.float32) / half)).astype(np.float32)
    pos = np.arange(KLEN, dtype=np.float32)
    ang = pos[None, :] * inv[:, None]
    cos = np.cos(ang).astype(np.float32)
    sin = np.sin(ang).astype(np.float32)
    c32 = np.concatenate([cos, cos], axis=0)
    s32 = np.concatenate([-sin, sin], axis=0)
    tab = np.stack([np.tile(c32, (3, 1)), np.tile(s32, (3, 1))], axis=0)
    return np.ascontiguousarray(tab, dtype=np.float32)


def kernel(**inputs):
    f = lambda a: np.ascontiguousarray(np.asarray(a), dtype=np.float32)
    x = f(inputs["x"])
    shared = {
        "meta": f(inputs["meta_tokens"]),
        "gains_pre": np.ascontiguousarray(np.concatenate(
            [f(inputs["ffn1_pre_g"]), f(inputs["mix_pre_g"]), f(inputs["ffn2_pre_g"])], axis=0)),
        "gains_post": np.ascontiguousarray(np.concatenate(
            [f(inputs["ffn1_post_g"]), f(inputs["mix_post_g"]), f(inputs["ffn2_post_g"])], axis=1)),
        "ffn1_w_gate": f(inputs["ffn1_w_gate"])[0], "ffn2_w_gate": f(inputs["ffn2_w_gate"])[0],
        "ffn1_w_up": f(inputs["ffn1_w_up"])[0], "ffn2_w_up": f(inputs["ffn2_w_up"])[0],
        "ffn1_w_down": f(inputs["ffn1_w_down"])[0], "ffn2_w_down": f(inputs["ffn2_w_down"])[0],
        "w_in": f(inputs["w_in"])[0],
        "b_gate": f(inputs["b_gate"]),
        "q_norm_g": f(inputs["q_norm_g"]),
        "kv_norm_g": f(inputs["kv_norm_g"]),
        "w_uq": f(inputs["w_uq"])[0],
        "w_ukv": f(inputs["w_ukv"])[0],
        "w_sb_o": f(inputs["w_sb_o"])[0],
        "w_mla_o": f(inputs["w_mla_o"])[0],
        "w_out": f(inputs["w_out"])[0],
        "rope_tab": _rope_table(),
    }
    nc = build_nc()
    in_maps = []
    for c in range(NCORES):
        m = dict(shared)
        m["x"] = np.ascontiguousarray(x[c * SEQ_PER_CORE:(c + 1) * SEQ_PER_CORE].reshape(XTOK, D))
        in_maps.append(m)
    res = run_bass_kernel_spmd(nc, in_maps, core_ids=list(range(NCORES)))
    out = np.stack([np.asarray(r["out"], dtype=np.float32).reshape(SEQ_PER_CORE, SEQ, D) for r in res.results], axis=0)
    return out.reshape(NCORES * SEQ_PER_CORE, SEQ, D)
```

```python
import numpy as np
from contextlib import ExitStack
import concourse.bass as bass
import concourse.mybir as mybir
from concourse.bass_utils import run_bass_kernel_spmd

F32 = mybir.dt.float32
BF16 = mybir.dt.bfloat16
AF = mybir.ActivationFunctionType
ALU = mybir.AluOpType

NCORES = 8
D = 1024
DFF = 2816
NFF = DFF // 128
SEQ = 2048
NMETA = 16
SEQ_PER_CORE = 2
XTOK = SEQ_PER_CORE * SEQ
G = 512
NG = XTOK // G
GPS = SEQ // G
KLEN = NMETA + SEQ
NVB = 1 + SEQ // 128
IN_COLS = 4256
EPS = 1e-6
MLA_SCALE = 96.0 ** -0.5
NSLOT = 6
CFG = {
    "sb_stages": [["Z"], ["A"], [], ["A2"], [], ["B"], [], ["C"], ["D"], ["E"]],
    "mla_stages": [["A"], ["B"], ["C"]],
    "order": "rev",
    "NE": 3, "NSP": 3, "NS": 2, "NA": 3, "NT": 3, "NP": 3,
}
SLOT_ELEMS = 2048


class Op:
    __slots__ = ("eng", "fn", "deps", "idx", "signal", "dma", "sigcount", "waits", "cost", "seq", "tag")


class Sched:
    ENGS = ("pe", "act", "dve", "pool", "sp")
    curtag = ""

    def __init__(self):
        self.ops = {e: [] for e in self.ENGS}
        self.lastw = {}
        self.readers = {}
        self.dmacnt = {}
        self.nseq = 0

    def simulate(self, semlat=120.0):
        allops = sorted((o for e in self.ENGS for o in self.ops[e]), key=lambda o: o.seq)
        t_eng = {e: 0.0 for e in self.ENGS}
        busy = {e: 0.0 for e in self.ENGS}
        done = {}
        self.dma_free = 0.0
        self.dma_busy = 0.0
        pred = {}
        starts = {}
        last_on = {}
        for op in allops:
            ready = 0.0
            rd = None
            for d in op.deps:
                if d.eng == "pe" and op.eng == "pe":
                    continue
                if done[id(d)] + semlat > ready:
                    ready = done[id(d)] + semlat
                    rd = d
            start = max(t_eng[op.eng], ready)
            if op.dma is not None:
                occ, nbytes = op.cost
                t_eng[op.eng] = start + occ
                busy[op.eng] += occ
                xs = max(start + occ, self.dma_free)
                self.dma_free = xs + nbytes / 300.0
                self.dma_busy += nbytes / 300.0
                done[id(op)] = self.dma_free + 2000.0
                pred[id(op)] = ("dep", rd) if ready > start - 1e-9 else ("eng", last_on.get(op.eng))
                last_on[op.eng] = op
                starts[id(op)] = start
                continue
            if ready > t_eng[op.eng]:
                pred[id(op)] = ("dep", rd)
            else:
                pred[id(op)] = ("eng", last_on.get(op.eng))
            last_on[op.eng] = op
            starts[id(op)] = start
            occ, lat = op.cost
            t_eng[op.eng] = start + occ
            busy[op.eng] += occ
            done[id(op)] = start + occ + lat
        self.sim_done = done
        self.sim_pred = pred
        self.sim_start = starts
        return max(done.values()), busy

    def add(self, eng, fn, reads=(), writes=(), dma=None, nodep=(), cost=None):
        op = Op()
        op.cost = cost if cost is not None else (600.0, 0.0)
        op.seq = self.nseq
        self.nseq += 1
        op.tag = Sched.curtag
        op.eng = eng
        op.fn = fn
        op.signal = False
        op.sigcount = 0
        op.dma = None
        deps = {}
        for r in reads:
            w = self.lastw.get(r)
            if w is not None:
                deps[id(w)] = w
            if r[0] == "ps":
                for rd in self.readers.get(r, ()):
                    if rd.eng != eng:
                        deps[id(rd)] = rd
        for r in writes:
            w = self.lastw.get(r)
            if w is not None:
                deps[id(w)] = w
            for rd in self.readers.get(r, ()):
                deps[id(rd)] = rd
        for r in reads:
            self.readers.setdefault(r, []).append(op)
        for r in writes:
            self.lastw[r] = op
            self.readers[r] = []
        for n in nodep:
            deps.pop(id(n), None)
        deps.pop(id(op), None)
        op.deps = list(deps.values())
        if dma is not None:
            c = self.dmacnt.get(dma, 0) + 16
            self.dmacnt[dma] = c
            op.dma = (dma, c)
        op.idx = len(self.ops[eng])
        self.ops[eng].append(op)
        return op

    def finalize(self):
        for eng in self.ENGS:
            wm = {}
            dwm = {}
            for op in self.ops[eng]:
                need = {}
                dwaits = []
                for d in op.deps:
                    if d.dma is not None:
                        name, val = d.dma
                        if dwm.get(name, 0) < val:
                            dwm[name] = val
                            dwaits.append((name, val))
                    else:
                        if d.eng == "pe" and eng == "pe":
                            continue
                        if wm.get(d.eng, -1) < d.idx:
                            if d.eng not in need or need[d.eng].idx < d.idx:
                                need[d.eng] = d
                for e2, d in need.items():
                    wm[e2] = d.idx
                    d.signal = True
                op.waits = (list(need.values()), dwaits)
        for eng in self.ENGS:
            cnt = 0
            for op in self.ops[eng]:
                if op.signal and op.dma is None:
                    cnt += 1
                    op.sigcount = cnt

    def run(self, e, eng, esem, dsem):
        for op in self.ops[eng]:
            for d in op.waits[0]:
                e.wait_ge(esem[d.eng], d.sigcount)
            for (name, val) in op.waits[1]:
                e.wait_ge(dsem[name], val)
            if op.fn is None:
                continue
            ins = op.fn(e)
            if op.dma is not None:
                ins.then_inc(dsem[op.dma[0]], 16)
            elif op.signal:
                ins.then_inc(esem[eng], 1)


def build_nc(dbg=None):
    nc = bass.Bass("TRN2", target_bir_lowering=False)

    def din(name, shape):
        return nc.dram_tensor(name, list(shape), F32, kind="ExternalInput").ap()

    x_d = din("x", [XTOK, D])
    meta_d = din("meta", [NMETA, D])
    gains_pre_d = din("gains_pre", [3, D])
    gains_post_d = din("gains_post", [1, 3 * D])
    wg_d = [din("ffn1_w_gate", [D, DFF]), din("ffn2_w_gate", [D, DFF])]
    wu_d = [din("ffn1_w_up", [D, DFF]), din("ffn2_w_up", [D, DFF])]
    wd_d = [din("ffn1_w_down", [DFF, D]), din("ffn2_w_down", [DFF, D])]
    win_d = din("w_in", [D, IN_COLS])
    bgate_d = din("b_gate", [1, 2 * D])
    qg_d = din("q_norm_g", [1, 384])
    kvg_d = din("kv_norm_g", [1, 256])
    wuq_d = din("w_uq", [384, 8, 96])
    wukv_d = din("w_ukv", [256, 8, 128])
    wsbo_d = din("w_sb_o", [512, D])
    wmlao_d = din("w_mla_o", [512, D])
    wout_d = din("w_out", [D, D])
    rope_d = din("rope_tab", [2, 96, KLEN])
    out_d = nc.dram_tensor("out", [XTOK, D], F32, kind="ExternalOutput").ap()

    S = Sched()
    es = ExitStack()
    with es:
        def sb(name, shape, dt=F32):
            return es.enter_context(nc.sbuf_tensor(name, list(shape), dt))

        ident = sb("ident", [128, 128], BF16)
        ones_bf = sb("ones_bf", [128, 128], BF16)
        ones_f = sb("ones_f", [128, 512], F32)
        maskc = sb("maskc", [128, 128], F32)
        negm = sb("negm", [128, 128], BF16)
        negm2 = sb("negm2", [128, 128], BF16)
        gpre = sb("gpre", [128, 3, 8], F32)
        gpost1 = sb("gpost1", [128, D], F32)
        bg = sb("bg", [128, 16], F32)
        qg = sb("qg", [128, 3], F32)
        kvg = sb("kvg", [128, 2], F32)
        cosT = sb("cosT", [96, G], F32)
        sinT = sb("sinT", [96, G], F32)
        h = sb("h", [128, 4, D], F32)
        xn = sb("xn", [128, D], BF16)
        xn2 = [xn, xn]
        stat = sb("stat", [128, 32], F32)
        uT = sb("uT", [128, 8, G], BF16)
        actT = sb("actT", [128, NFF, G], BF16)
        slots = [sb(f"slot{i}", [128, SLOT_ELEMS], BF16) for i in range(NSLOT)]
        qm_sb = sb("qm_sb", [128, 8, G], BF16)
        kT_sb = sb("kT_sb", [128, 4, KLEN], BF16)
        v_sb = sb("v_sb", [128, NVB, 512], BF16)
        qm_nope = sb("qm_nope", [128, 8, G], BF16)
        qm_rope = sb("qm_rope", [96, 8, G], BF16)
        kT_nope = sb("kT_nope", [128, 4, KLEN], BF16)
        kT_rope = sb("kT_rope", [96, KLEN], BF16)
        v_mla = sb("v_mla", [128, NVB, 8, 65], BF16)
        NE, NS, NA, NT_, NP, NSP = CFG["NE"], CFG["NS"], CFG["NA"], CFG["NT"], CFG["NP"], CFG["NSP"]
        EB = sb("EB", [128, NE, 512], F32)
        Ebuf = [EB[:, i, :] for i in range(NE)]
        ptmp = EB[:, 0:2, :].rearrange("p a b -> p (a b)")
        identf = EB[:, 0, 0:128]
        mt1 = Ebuf[0]
        mt2 = Ebuf[1]
        Sfx = [sb(f"Sfx{i}", [128, 512], F32) for i in range(NS)]
        SPB = sb("SPB", [128, NSP, 512], F32)
        SPbuf = [SPB[:, i, :] for i in range(NSP)]
        ptmp2 = SPB[:, 0:2, :].rearrange("p a b -> p (a b)")
        abuf = [sb(f"abuf{i}", [128, 512], BF16) for i in range(NA)]
        aT = [sb(f"aT{i}", [128, 4, 128], BF16) for i in range(NT_)]
        pT = [sb(f"pT{i}", [128, 512], BF16) for i in range(NP)]
        ytok = sb("ytok", [128, 512], BF16)
        ytok2 = sb("ytok2", [128, 512], BF16)
        rc = sb("rc", [128, 8], F32)
        mixT = actT[:, 0:8, :]
        yT_sb = actT[:, 8:12, :]
        yT_mla = actT[:, 12:16, :]
        cqT = actT[:, 16:19, :]
        ckvT = actT[:, 19:21, :]
        K_ysb = [("actT", k) for k in range(8, 12)]
        K_ymla = [("actT", k) for k in range(12, 16)]
        K_cq = [("actT", k) for k in range(16, 19)]
        K_ckv = [("actT", k) for k in range(19, 21)]
        rt1 = Sfx[0][0:96, :]
        rt2 = Sfx[1][0:96, :]
        sg = [abuf[0], abuf[1]]
        gs = pT[0]
        gm = pT[1]
        rq_rep = Ebuf[0]
        rkv_rep = Ebuf[1]
        ps = es.enter_context(nc.psum_tensor("ps", [128, 8, 512], F32))

        csem = es.enter_context(nc.semaphore("csem"))
        psem = es.enter_context(nc.semaphore("psem"))
        with nc.Block() as block:
            @block.sync
            def _(e):
                n = 0
                def col(dst, src_row, c):
                    return e.dma_start(out=dst, in_=src_row[:, c * 128:(c + 1) * 128].rearrange("a p -> p a"),
                                       allow_slow_non_contiguous=True)
                for a in range(3):
                    for c in range(8):
                        col(gpre[:, a, c:c + 1], gains_pre_d[a:a + 1, :], c).then_inc(csem, 16); n += 16
                for c in range(16):
                    col(bg[:, c:c + 1], bgate_d, c).then_inc(csem, 16); n += 16
                for c in range(3):
                    col(qg[:, c:c + 1], qg_d, c).then_inc(csem, 16); n += 16
                for c in range(2):
                    col(kvg[:, c:c + 1], kvg_d, c).then_inc(csem, 16); n += 16
                e.wait_ge(csem, n)

            @block.gpsimd
            def _(e):
                e.memset(identf, 1.0).then_inc(psem, 1)
                e.memset(maskc[:], 1.0).then_inc(psem, 1)
                e.wait_ge(psem, 2)
                e.affine_select(out=identf, in_=identf, pattern=[[-1, 128]],
                                compare_op=ALU.is_equal, fill=0.0, base=0, channel_multiplier=1).then_inc(psem, 1)
                e.affine_select(out=maskc[:], in_=maskc[:], pattern=[[-1, 128]],
                                compare_op=ALU.is_ge, fill=0.0, base=-1, channel_multiplier=1).then_inc(psem, 1)
                e.wait_ge(psem, 4)
                e.tensor_scalar(out=negm[:], in0=maskc[:], scalar1=-1.0, scalar2=30000.0, op0=ALU.add, op1=ALU.mult)
                e.memset(negm2[:], 0.0).then_inc(psem, 1)
                e.wait_ge(psem, 5)
                e.memset(negm2[64:128, 0:64], -30000.0)
                e.tensor_copy(out=ident[:], in_=identf)
                e.memset(ones_bf[:], 1.0)
                e.memset(ones_f[:], 1.0)
                e.memset(v_mla[:].rearrange("p a b c -> p (a b c)"), 1.0)
                e.memset(h[:].rearrange("p a d -> p (a d)"), 0.0)
                e.memset(qm_sb[:].rearrange("p a d -> p (a d)"), 0.0)
                e.memset(qm_nope[:].rearrange("p a d -> p (a d)"), 0.0)
                e.memset(qm_rope[:].rearrange("p a d -> p (a d)"), 0.0)

        bank_ctr = [0]

        held = set()

        def bank():
            for _ in range(16):
                b = bank_ctr[0] % 8
                bank_ctr[0] += 1
                if b not in held:
                    return b
            raise RuntimeError("all PSUM banks held")

        def bank_pair():
            while True:
                if bank_ctr[0] % 2:
                    bank_ctr[0] += 1
                b = bank_ctr[0] % 8
                bank_ctr[0] += 2
                if b not in held and (b + 1) not in held:
                    return b

        slot_ctr = [0]

        def wload(pieces):
            k = slot_ctr[0] % NSLOT
            slot_ctr[0] += 1
            st = slots[k]
            prev = []
            for (dfn, src) in pieces:
                dst = dfn(st)
                op = S.add("pool", (lambda e, dst=dst, src=src: e.dma_start(out=dst, in_=src)),
                           writes=[("slot", k)], dma=f"slot{k}", nodep=prev, cost=(1000.0, dst.size() * 4.0))
                prev.append(op)
            return st, ("slot", k)

        def v3(st, a, b):
            return st[:, 0:a * b].rearrange("p (a b) -> p a b", a=a)

        COLD = [False]

        def pe_cost(n):
            return (max(64.0, n / 1.2), 110.0) if COLD[0] else (max(40.0, n / 2.4 + 12), 60.0)

        def mm(out, lhsT, rhs, start, stop, reads, writes, skip=False):
            cst = pe_cost(out.free_size())
            if skip:
                S.add("pe", (lambda e: e.matmul(out, lhsT=lhsT, rhs=rhs, start=start, stop=stop, skip_group_check=True)), reads, writes, cost=cst)
            else:
                S.add("pe", (lambda e: e.matmul(out, lhsT=lhsT, rhs=rhs, start=start, stop=stop)), reads, writes, cost=cst)

        def tr(out, in_, idn, reads, writes):
            S.add("pe", (lambda e: e.transpose(out, in_, idn)), reads, writes, cost=(107.0, 45.0) if COLD[0] else (56.0, 45.0))

        def act(out, in_, func, reads, writes, **kw):
            cst = (224.0 + 0.7 * in_.free_size(), 0.0)
            if globals().get("SIM_FREE_LN") and (func == AF.Ln or kw.get("scale") == -1.0):
                cst = (1.0, 0.0)
            S.add("act", (lambda e: e.activation(out=out, in_=in_, func=func, **kw)), reads, writes, cost=cst)

        def dve(fn, reads, writes, cost=None):
            S.add("dve", fn, reads, writes, cost=cost)

        def psbf(b):
            return ps[:, b, :].bitcast(BF16)

        def stage_load(g):
            if g < 0:
                S.add("sp", (lambda e: e.dma_start(out=h[0:NMETA, 0, :], in_=meta_d[:, :])),
                      writes=[("h", 0)], dma="hl0")
                return
            for i in range(4):
                r0 = g * G + i * 128
                S.add("sp", (lambda e, i=i, r0=r0: e.dma_start(out=h[:, i, :], in_=x_d[r0:r0 + 128, :])),
                      writes=[("h", i)], dma=f"hl{i}", cost=(100.0, 524288.0))

        def stage_rope_tab(g):
            kp = 0 if g < 0 else NMETA + (g % GPS) * G
            n = NMETA if g < 0 else G
            S.add("sp", (lambda e: e.dma_start(out=cosT[:, 0:n], in_=rope_d[0, :, kp:kp + n])),
                  writes=[("cosT",)], dma="ropec")
            S.add("sp", (lambda e: e.dma_start(out=sinT[:, 0:n], in_=rope_d[1, :, kp:kp + n])),
                  writes=[("sinT",)], dma="ropes")

        def stage_norm(tiles, gi):
            for i, (t0, r) in enumerate(tiles):
                junk = uT[0:r, 2 * i:2 * i + 2, :].rearrange("p a b -> p (a b)")
                act(junk, h[0:r, i, :], AF.Square, [("h", i)], [("uT",), ("st", 8 + i)], accum_out=stat[0:r, 8 + i:9 + i])
            for i, (t0, r) in enumerate(tiles):
                act(stat[0:r, 12 + i:13 + i], stat[0:r, 8 + i:9 + i], AF.Sqrt, [("st", 8 + i)], [("st", 12 + i)], scale=1.0 / D, bias=EPS)
            for i, (t0, r) in enumerate(tiles):
                rs = stat[0:r, 12 + i:13 + i]
                dve((lambda e, rs=rs: e.reciprocal(out=rs, in_=rs)), [("st", 12 + i)], [("st", 12 + i)])
            for i, (t0, r) in enumerate(tiles):
                rs = stat[0:r, 12 + i:13 + i]
                xb_ = xn2[i % 2]
                act(xb_[0:r, :], h[0:r, i, :], AF.Copy, [("h", i), ("st", 12 + i)], [("xn", 0)], scale=rs)
                b = bank()
                pv = psbf(b)
                for kc in range(8):
                    tr(pv[:, kc * 128:kc * 128 + r], xb_[0:r, kc * 128:(kc + 1) * 128], ident[0:r, 0:r],
                       [("xn", 0)], [("ps", b)])
                src = pv.rearrange("p (c t) -> p c t", c=8)[:, :, 0:r]
                dst = uT[:, :, t0:t0 + r]
                gv = gpre[:, gi, :].unsqueeze(2).to_broadcast([128, 8, r])
                dve((lambda e, dst=dst, src=src, gv=gv: e.tensor_tensor(out=dst, in0=src, in1=gv, op=ALU.mult)),
                    [("ps", b)], [("uT",)])

        def stage_ffn(tiles, f, tok):
            wg, wu, wd = wg_d[f], wu_d[f], wd_d[f]
            load_gpost(0 if f == 0 else 2)
            for blk in range(NFF // 2):
                c0 = blk * 256
                sa, ka = wload([(lambda st: v3(st, 8, 256), wg[:, c0:c0 + 256].rearrange("(kc p) n -> p kc n", p=128))])
                sbv, kb = wload([(lambda st: v3(st, 8, 256), wu[:, c0:c0 + 256].rearrange("(kc p) n -> p kc n", p=128))])
                A = v3(sa, 8, 256)
                B = v3(sbv, 8, 256)
                for j in range(2):
                    ffc = 2 * blk + j
                    bgt = bank()
                    for kc in range(8):
                        mm(ps[:, bgt, 0:tok], A[:, kc, j * 128:(j + 1) * 128], uT[:, kc, 0:tok], kc == 0, kc == 7,
                           [ka, ("uT",)], [("ps", bgt)])
                    bu = bank()
                    for kc in range(8):
                        mm(ps[:, bu, 0:tok], B[:, kc, j * 128:(j + 1) * 128], uT[:, kc, 0:tok], kc == 0, kc == 7,
                           [kb, ("uT",)], [("ps", bu)])
                    sgi = ffc % 2
                    act(sg[sgi][:, 0:tok], ps[:, bgt, 0:tok], AF.Silu, [("ps", bgt)], [("a", sgi)])
                    dve((lambda e, o=actT[:, ffc, 0:tok], a=ps[:, bu, 0:tok], b2=sg[sgi][:, 0:tok]:
                         e.tensor_tensor(out=o, in0=a, in1=b2, op=ALU.mult)),
                        [("ps", bu), ("a", sgi)], [("actT", ffc)])
            bank_ctr[0] = 0
            for blk in range(NFF // 2):
                r0 = blk * 256
                sw, kw = wload([(lambda st: v3(st, 2, D), wd[r0:r0 + 256, :].rearrange("(j p) n -> p j n", p=128))])
                W = v3(sw, 2, D)
                for j in range(2):
                    ffc = 2 * blk + j
                    for i, (t0, r) in enumerate(tiles):
                        for nh in range(2):
                            b = 2 * i + nh
                            mm(ps[0:r, b, :], actT[:, ffc, t0:t0 + r], W[:, j, nh * 512:(nh + 1) * 512],
                               ffc == 0, ffc == NFF - 1, [kw, ("actT", ffc)], [("ps", b)])
            stage_post(tiles, 0 if f == 0 else 2, half=True)
            bank_ctr[0] = 2 * len(tiles)

        def load_gpost(gi):
            S.add("sp", (lambda e: e.dma_start(out=gpost1[:, :], in_=gains_post_d[0:1, gi * D:(gi + 1) * D].partition_broadcast(128))),
                  writes=[("gpost",)], dma="gp", cost=(100.0, 524288.0))

        def stage_post(tiles, gi, half):
            k = 4.0 if half else 1.0
            for i, (t0, r) in enumerate(tiles):
                fps = ps[0:r, 2 * i:2 * i + 2, :]
                junk = actT[0:r, 2 * i:2 * i + 2, :].rearrange("p a b -> p (a b)")
                act(junk, fps, AF.Square, [("ps", 2 * i), ("ps", 2 * i + 1)],
                    [("actT", 2 * i), ("actT", 2 * i + 1), ("st", 16 + i)], accum_out=stat[0:r, 16 + i:17 + i])
            for i, (t0, r) in enumerate(tiles):
                act(stat[0:r, 20 + i:21 + i], stat[0:r, 16 + i:17 + i], AF.Sqrt, [("st", 16 + i)], [("st", 20 + i)],
                    scale=k / D, bias=k * EPS)
            for i, (t0, r) in enumerate(tiles):
                rs = stat[0:r, 20 + i:21 + i]
                dve((lambda e, rs=rs: e.reciprocal(out=rs, in_=rs)), [("st", 20 + i)], [("st", 20 + i)])
            for i, (t0, r) in enumerate(tiles):
                fps = ps[0:r, 2 * i:2 * i + 2, :]
                rs = stat[0:r, 20 + i:21 + i]
                pt, pk = (ptmp, [("E", 0), ("E", 1)]) if i % 2 == 0 else (ptmp2, [("SP", 0), ("SP", 1)])
                dve((lambda e, fps=fps, rs=rs, r=r, pt=pt: e.scalar_tensor_tensor(
                    out=pt[0:r, :], in0=fps, scalar=rs, in1=gpost1[0:r, :], op0=ALU.mult, op1=ALU.mult)),
                    [("ps", 2 * i), ("ps", 2 * i + 1), ("st", 20 + i), ("gpost",)], pk, cost=(1300.0, 0.0))
                dve((lambda e, i=i, r=r, pt=pt: e.tensor_tensor(out=h[0:r, i, :], in0=h[0:r, i, :], in1=pt[0:r, :], op=ALU.add)),
                    pk + [("h", i)], [("h", i)], cost=(1000.0, 0.0))

        def win_cols(c0, n):
            return win_d[:, c0:c0 + n].rearrange("(kc p) n -> p kc n", p=128)

        def stage_proj(tiles, tok, g):
            is_meta = g < 0
            gi = "m" if is_meta else g % GPS
            kp = 0 if is_meta else NMETA + (g % GPS) * G
            U = [("uT",)]
            PP = 99 if (dbg is None or is_meta) else dbg.get("pp", 99)
            for which in ([1] if is_meta else [0, 1]):
                for half in range(2):
                    c0 = which * 512 + half * 256
                    st, kk = wload([(lambda s_: v3(s_, 8, 256), win_cols(c0, 256))])
                    W = v3(st, 8, 256)
                    for j in range(2):
                        c = half * 2 + j
                        b = bank()
                        for kc in range(8):
                            mm(ps[:, b, 0:tok], W[:, kc, j * 128:(j + 1) * 128], uT[:, kc, 0:tok], kc == 0, kc == 7,
                               [kk] + U, [("ps", b)])
                        if which == 0:
                            act(qm_sb[0:64, 2 * c, 0:tok], ps[0:64, b, 0:tok], AF.Copy, [("ps", b)], [("qT_sb",)], scale=0.125)
                            act(qm_sb[64:128, 2 * c + 1, 0:tok], ps[64:128, b, 0:tok], AF.Copy, [("ps", b)], [("qT_sb",)], scale=0.125)
                        else:
                            act(kT_sb[:, c, kp:kp + tok], ps[:, b, 0:tok], AF.Copy, [("ps", b)], [("kT_sb", gi)])
            if PP <= 1:
                return
            sv = []
            for half in range(2):
                st, kk = wload([(lambda s_: v3(s_, 8, 256), win_cols(1024 + half * 256, 256))])
                sv.append((v3(st, 8, 256), kk))
            for i, (t0, r) in enumerate(tiles):
                b = bank()
                for half in range(2):
                    W, kk = sv[half]
                    for kc in range(8):
                        mm(ps[0:r, b, half * 256:(half + 1) * 256], uT[:, kc, t0:t0 + r], W[:, kc, :], kc == 0, kc == 7,
                           [kk] + U, [("ps", b)])
                vb = 0 if is_meta else 1 + (g % GPS) * 4 + i
                dve((lambda e, o=v_sb[0:r, vb, :], a=ps[0:r, b, :]: e.tensor_copy(out=o, in_=a)),
                    [("ps", b)], [("v_sb", gi)])
            if PP <= 2:
                return
            if not is_meta:
                st1, k1 = wload([(lambda s_: v3(s_, 8, 256), win_cols(1536, 256))])
                st2, k2 = wload([(lambda s_: v3(s_, 8, 256), win_cols(1792, 256))])
                lw = [(v3(st1, 8, 256), 0, k1), (v3(st1, 8, 256), 128, k1), (v3(st2, 8, 256), 0, k2)]
                for c in range(3):
                    W, off, kk = lw[c]
                    b = bank()
                    for kc in range(8):
                        mm(ps[:, b, 0:tok], W[:, kc, off:off + 128], uT[:, kc, 0:tok], kc == 0, kc == 7, [kk] + U, [("ps", b)])
                    dve((lambda e, o=cqT[:, c, 0:tok], a=ps[:, b, 0:tok], s=qg[:, c:c + 1]:
                         e.tensor_scalar(out=o, in0=a, scalar1=s, scalar2=None, op0=ALU.mult)),
                        [("ps", b)], K_cq + [("ser",)])
                    act(pT[c][:, 0:tok], ps[:, b, 0:tok], AF.Square, [("ps", b)], [("pT", c), ("ser",)])
                if PP == 3 and dbg.get("sub", 0) == 1:
                    return
                b = bank()
                for c in range(3):
                    mm(ps[:, b, 0:tok], ones_bf[:, :], pT[c][:, 0:tok], c == 0, c == 2, [("pT", c)], [("ps", b)])
                act(rq_rep[:, 0:tok], ps[:, b, 0:tok], AF.Sqrt, [("ps", b)], [("E", 0)], scale=1.0 / 384, bias=EPS)
                dve((lambda e: e.reciprocal(out=rq_rep[:, 0:tok], in_=rq_rep[:, 0:tok])), [("E", 0)], [("E", 0)])
            if PP <= 3:
                return
            st1, k1 = wload([(lambda s_: v3(s_, 8, 256), win_cols(1920, 256))])
            W = v3(st1, 8, 256)
            for c in range(2):
                b = bank()
                for kc in range(8):
                    mm(ps[:, b, 0:tok], W[:, kc, c * 128:(c + 1) * 128], uT[:, kc, 0:tok], kc == 0, kc == 7, [k1] + U, [("ps", b)])
                dve((lambda e, o=ckvT[:, c, 0:tok], a=ps[:, b, 0:tok], s=kvg[:, c:c + 1]:
                     e.tensor_scalar(out=o, in0=a, scalar1=s, scalar2=None, op0=ALU.mult)),
                    [("ps", b)], K_ckv + [("ser",)])
                act(pT[c][:, 0:tok], ps[:, b, 0:tok], AF.Square, [("ps", b)], [("pT", c), ("ser",)])
            b = bank()
            for c in range(2):
                mm(ps[:, b, 0:tok], ones_bf[:, :], pT[c][:, 0:tok], c == 0, c == 1, [("pT", c)], [("ps", b)])
            act(rkv_rep[:, 0:tok], ps[:, b, 0:tok], AF.Sqrt, [("ps", b)], [("E", 1)], scale=1.0 / 256, bias=EPS)
            dve((lambda e: e.reciprocal(out=rkv_rep[:, 0:tok], in_=rkv_rep[:, 0:tok])), [("E", 1)], [("E", 1)])
            for i, (t0, r) in enumerate(tiles):
                b = bank()
                for c in range(2):
                    mm(ps[0:r, b, 0:1], pT[c][:, t0:t0 + r], ones_bf[:, 0:1], c == 0, c == 1, [("pT", c)], [("ps", b)])
                rs = stat[0:r, 4 + i:5 + i]
                act(rs, ps[0:r, b, 0:1], AF.Sqrt, [("ps", b)], [("st", 4 + i)], scale=1.0 / 256, bias=EPS)
                dve((lambda e, rs=rs: e.reciprocal(out=rs, in_=rs)), [("st", 4 + i)], [("st", 4 + i)])
            if PP <= 5:
                return
            pieces = []
            for rep in range(3):
                pieces.append((lambda s_, rep=rep: v3(s_, 8, 192)[:, :, rep * 32:(rep + 1) * 32], win_cols(2176, 32)))
                pieces.append((lambda s_, rep=rep: v3(s_, 8, 192)[:, :, 96 + rep * 32:96 + rep * 32 + 16], win_cols(2192, 16)))
                pieces.append((lambda s_, rep=rep: v3(s_, 8, 192)[:, :, 96 + rep * 32 + 16:96 + rep * 32 + 32], win_cols(2176, 16)))
            st, kk = wload(pieces)
            W = v3(st, 8, 192)
            braw = bank()
            for kc in range(8):
                mm(ps[0:96, braw, 0:tok], W[:, kc, 0:96], uT[:, kc, 0:tok], kc == 0, kc == 7, [kk] + U, [("ps", braw)])
            brot = bank()
            for kc in range(8):
                mm(ps[0:96, brot, 0:tok], W[:, kc, 96:192], uT[:, kc, 0:tok], kc == 0, kc == 7, [kk] + U, [("ps", brot)])
            dve((lambda e, braw=braw: e.tensor_tensor(out=rt1[:, 0:tok], in0=ps[0:96, braw, 0:tok], in1=cosT[:, 0:tok], op=ALU.mult)),
                [("ps", braw), ("cosT",)], [("Sfx", 0)])
            dve((lambda e, brot=brot: e.tensor_tensor(out=rt2[:, 0:tok], in0=ps[0:96, brot, 0:tok], in1=sinT[:, 0:tok], op=ALU.mult)),
                [("ps", brot), ("sinT",)], [("Sfx", 1)])
            dve((lambda e: e.tensor_tensor(out=kT_rope[:, kp:kp + tok], in0=rt1[:, 0:tok], in1=rt2[:, 0:tok], op=ALU.add)),
                [("Sfx", 0), ("Sfx", 1)], [("kT_rope", gi)])
            if PP <= 6:
                return
            if not is_meta:
                stn, kn = wload([(lambda s_, kc=kc: s_[:, kc * 512:(kc + 1) * 512].rearrange("p (h d) -> p h d", h=8),
                                  wuq_d[kc * 128:(kc + 1) * 128, :, 0:64]) for kc in range(3)])
                Wn = stn[:, 0:1536].rearrange("p (kc hd) -> p kc hd", kc=3)
                for c in range(4):
                    b = bank()
                    for kc in range(3):
                        mm(ps[:, b, 0:tok], Wn[:, kc, c * 128:(c + 1) * 128], cqT[:, kc, 0:tok], kc == 0, kc == 2,
                           [kn] + K_cq, [("ps", b)])
                    dve((lambda e, o=qm_nope[0:64, 2 * c, 0:tok], a=ps[0:64, b, 0:tok]:
                         e.tensor_tensor(out=o, in0=a, in1=rq_rep[0:64, 0:tok], op=ALU.mult)),
                        [("ps", b), ("E", 0)], [("qT_nope",)])
                    dve((lambda e, o=qm_nope[64:128, 2 * c + 1, 0:tok], a=ps[64:128, b, 0:tok]:
                         e.tensor_tensor(out=o, in0=a, in1=rq_rep[64:128, 0:tok], op=ALU.mult)),
                        [("ps", b), ("E", 0)], [("qT_nope",)])
                if PP <= 7:
                    return
                def rv(s_, base, kc):
                    return s_[:, base + kc * 256:base + (kc + 1) * 256].rearrange("p (h d) -> p h d", h=8)
                rp = []
                for kc in range(3):
                    rws = slice(kc * 128, (kc + 1) * 128)
                    rp.append((lambda s_, kc=kc: rv(s_, 0, kc), wuq_d[rws, :, 64:96]))
                    rp.append((lambda s_, kc=kc: rv(s_, 768, kc)[:, :, 0:16], wuq_d[rws, :, 80:96]))
                    rp.append((lambda s_, kc=kc: rv(s_, 768, kc)[:, :, 16:32], wuq_d[rws, :, 64:80]))
                str_, kr = wload(rp)
                Wr = str_[:, 0:768].rearrange("p (kc hd) -> p kc hd", kc=3)
                Wt = str_[:, 768:1536].rearrange("p (kc hd) -> p kc hd", kc=3)
                for gq in range(3):
                    h0 = gq * 3
                    nh_ = 3 if gq < 2 else 2
                    m = nh_ * 32
                    braw = bank()
                    for kc in range(3):
                        mm(ps[0:m, braw, 0:tok], Wr[:, kc, h0 * 32:h0 * 32 + m], cqT[:, kc, 0:tok], kc == 0, kc == 2,
                           [kr] + K_cq, [("ps", braw)])
                    brot = bank()
                    for kc in range(3):
                        mm(ps[0:m, brot, 0:tok], Wt[:, kc, h0 * 32:h0 * 32 + m], cqT[:, kc, 0:tok], kc == 0, kc == 2,
                           [kr] + K_cq, [("ps", brot)])
                    dve((lambda e, m=m, braw=braw: e.tensor_tensor(out=rt1[0:m, 0:tok], in0=ps[0:m, braw, 0:tok], in1=cosT[0:m, 0:tok], op=ALU.mult)),
                        [("ps", braw), ("cosT",)], [("Sfx", 0)])
                    dve((lambda e, m=m, brot=brot: e.tensor_tensor(out=rt2[0:m, 0:tok], in0=ps[0:m, brot, 0:tok], in1=sinT[0:m, 0:tok], op=ALU.mult)),
                        [("ps", brot), ("sinT",)], [("Sfx", 1)])
                    dve((lambda e, m=m: e.tensor_tensor(out=rt1[0:m, 0:tok], in0=rt1[0:m, 0:tok], in1=rt2[0:m, 0:tok], op=ALU.add)),
                        [("Sfx", 0), ("Sfx", 1)], [("Sfx", 0)])
                    for jh in range(nh_):
                        rr = slice(32 * jh, 32 * jh + 32)
                        dve((lambda e, rr=rr, hq=h0 + jh: e.tensor_tensor(out=qm_rope[rr, hq, 0:tok], in0=rt1[rr, 0:tok], in1=rq_rep[rr, 0:tok], op=ALU.mult)),
                            [("Sfx", 0), ("E", 0)], [("qT_rope",)])
            if PP <= 8:
                return
            stk, kkk = wload([(lambda s_, kc=kc: s_[:, kc * 512:(kc + 1) * 512].rearrange("p (h d) -> p h d", h=8),
                               wukv_d[kc * 128:(kc + 1) * 128, :, 0:64]) for kc in range(2)])
            Wk = stk[:, 0:1024].rearrange("p (kc hd) -> p kc hd", kc=2)
            for c in range(4):
                b = bank()
                for kc in range(2):
                    mm(ps[:, b, 0:tok], Wk[:, kc, c * 128:(c + 1) * 128], ckvT[:, kc, 0:tok], kc == 0, kc == 1,
                       [kkk] + K_ckv, [("ps", b)])
                dve((lambda e, o=kT_nope[:, c, kp:kp + tok], a=ps[:, b, 0:tok]:
                     e.tensor_tensor(out=o, in0=a, in1=rkv_rep[:, 0:tok], op=ALU.mult)),
                    [("ps", b), ("E", 1)], [("kT_nope", gi)])
            stv, kv_ = wload([(lambda s_, kc=kc: s_[:, kc * 512:(kc + 1) * 512].rearrange("p (h d) -> p h d", h=8),
                               wukv_d[kc * 128:(kc + 1) * 128, :, 64:128]) for kc in range(2)])
            Wv = stv[:, 0:1024].rearrange("p (kc hd) -> p kc hd", kc=2)
            for i, (t0, r) in enumerate(tiles):
                b = bank()
                for kc in range(2):
                    mm(ps[0:r, b, :], ckvT[:, kc, t0:t0 + r], Wv[:, kc, :], kc == 0, kc == 1, [kv_] + K_ckv, [("ps", b)])
                vb = 0 if is_meta else 1 + (g % GPS) * 4 + i
                act(v_mla[0:r, vb, :, 0:64], ps[0:r, b, :].rearrange("p (h d) -> p h d", h=8), AF.Copy,
                    [("ps", b), ("st", 4 + i)], [("v_mla", gi)], scale=stat[0:r, 4 + i:5 + i])

        def key_blocks(j):
            bl = [(0, NMETA, 0)]
            for m in range(j + 1):
                bl.append((NMETA + 128 * m, 128, 1 + m))
            return bl

        rotE = [0]
        rotS = [0]
        rotA = [0]
        rotT = [0]
        rotP = [0]
        rotQ = [0]
        NSTG = max(len(CFG["sb_stages"]), len(CFG["mla_stages"]))

        def run_pipeline(units):
            n = len(units)
            for k in range(n + NSTG - 1):
                if CFG["order"] == "rev":
                    sorder = list(range(NSTG - 1, -1, -1))
                elif CFG["order"] == "fwd":
                    sorder = list(range(NSTG))
                else:
                    sorder = [0] + list(range(NSTG - 1, 0, -1))
                for sidx in sorder:
                    u = k - sidx
                    if 0 <= u < n and units[u][sidx] is not None:
                        units[u][sidx]()

        def sb_unit(ctx, hd, ch, is_last_chunk, first_flag, last_flag, kres, vres, fin):
            t0 = ctx["t0"]
            c = hd // 2
            pb = 64 * (hd % 2)
            k0 = ch[0][0]
            kn = sum(w for (_, w, _) in ch)
            nb = len(ch)
            st = {}

            def sA():
                rE = rotE[0] % NE
                rotE[0] += 1
                rQ = rotQ[0] % NSP
                rotQ[0] += 1
                st["rE"] = rE
                st["rQ"] = rQ
                E = Ebuf[rE]
                SPv = SPbuf[rQ]
                if "Z" not in [n_ for g_ in CFG["sb_stages"] for n_ in g_]:
                    sZ()
                zb = st["zb"]
                act(E[:, 0:kn], ps[:, zb, 0:kn], AF.Exp, [("ps", zb)], [("E", rE)])
                held.discard(zb)
                act(SPv[:, 0:kn], E[:, 0:kn], AF.Ln, [("E", rE)], [("SP", rQ)], bias=1.0)

            def sZ():
                if ctx["ob"] is None:
                    ctx["ob"] = bank()
                    held.add(ctx["ob"])
                zb = bank()
                held.add(zb)
                st["zb"] = zb
                mm(ps[:, zb, 0:kn], qm_sb[:, hd, t0:t0 + 128], kT_sb[:, c, k0:k0 + kn], True, not is_last_chunk,
                   [("qT_sb",)] + kres, [("ps", zb)])
                if is_last_chunk:
                    mm(ps[:, zb, kn - 128:kn], ident[:, :], negm[:, :], False, True, [], [("ps", zb)])

            def sA2():
                rQ = st["rQ"]
                SPv = SPbuf[rQ]
                rS = rotS[0] % NS
                rotS[0] += 1
                st["rS"] = rS
                carry = ctx["carry"].get(hd)
                init = 0.0 if carry is None else carry[0]
                rd = [("SP", rQ)] + ([] if carry is None else [carry[1]])
                dve((lambda e: e.tensor_tensor_scan(
                    out=Sfx[rS][:, 0:kn][:, ::-1], data0=ones_f[:, 0:kn], data1=SPv[:, 0:kn][:, ::-1],
                    initial=init, op0=ALU.mult, op1=ALU.add)), rd, [("Sfx", rS)], cost=(2.25 * kn, 0.0))
                ctx["carry"][hd] = (Sfx[rS][:, 0:1], ("Sfx", rS))

            def sB():
                rE, rQ, rS = st["rE"], st["rQ"], st["rS"]
                E = Ebuf[rE]
                SPv = SPbuf[rQ]
                rA = rotA[0] % NA
                rotA[0] += 1
                st["rA"] = rA
                A = abuf[rA]
                act(SPv[:, 0:kn], Sfx[rS][:, 0:kn], AF.Exp, [("Sfx", rS)], [("SP", rQ)], scale=-1.0)
                S.add("pool", (lambda e: e.tensor_tensor(out=A[:, 0:kn], in0=E[:, 0:kn], in1=SPv[:, 0:kn], op=ALU.mult)),
                      [("E", rE), ("SP", rQ)], [("a", rA)], cost=(150.0 + 1.72 * kn, 0.0))

            def sC():
                rA = st["rA"]
                A = abuf[rA]
                tb = bank()
                held.add(tb)
                st["tb"] = tb
                pv = psbf(tb)
                off = 0
                for bi, (kcol, w, vb) in enumerate(ch):
                    tr(pv[0:w, bi * 128:(bi + 1) * 128], A[:, off:off + w], ident[:, :], [("a", rA)], [("ps", tb)])
                    off += w

            def sD():
                tb = st["tb"]
                rT = rotT[0] % NT_
                rotT[0] += 1
                st["rT"] = rT
                pv = psbf(tb)
                b0 = 0
                if ch[0][1] < 128:
                    w0 = ch[0][1]
                    dve((lambda e: e.tensor_copy(out=aT[rT][0:w0, 0, :], in_=pv[0:w0, 0:128])),
                        [("ps", tb)], [("aT", rT)], cost=(120.0, 0.0))
                    b0 = 1
                if nb > b0:
                    dve((lambda e: e.tensor_copy(out=aT[rT][:, b0:nb, :].rearrange("p a b -> p (a b)"), in_=pv[:, b0 * 128:nb * 128])),
                        [("ps", tb)], [("aT", rT)], cost=(60.0 + 0.48 * (nb - b0) * 128, 0.0))
                held.discard(tb)

            def sE():
                rT = st["rT"]
                ob = ctx["ob"]
                for bi, (kcol, w, vb) in enumerate(ch):
                    mm(ps[:, ob, hd * 64:(hd + 1) * 64], aT[rT][0:w, bi, :], v_sb[0:w, vb, hd * 64:(hd + 1) * 64],
                       first_flag and bi == 0, last_flag and bi == nb - 1, [("aT", rT)] + vres, [("ps", ob)])
                if fin:
                    dve((lambda e: e.tensor_copy(out=ytok[:, :], in_=ps[:, ob, :])), [("ps", ob)], [("ytok",)])
                    held.discard(ob)
                    tb = bank()
                    pv = psbf(tb)
                    for cc in range(4):
                        tr(pv[:, cc * 128:(cc + 1) * 128], ytok[:, cc * 128:(cc + 1) * 128], ident[:, :], [("ytok",)], [("ps", tb)])
                    act(actT[:, 8:12, t0:t0 + 128], pv[:, 0:512].rearrange("p (c t) -> p c t", c=4), AF.Copy, [("ps", tb)], K_ysb)

            prim = {"Z": sZ, "A": sA, "A2": sA2, "B": sB, "C": sC, "D": sD, "E": sE}

            def mk(names):
                def f():
                    for n_ in names:
                        Sched.curtag = f"sb{ctx['t0'] // 128}h{hd}k{k0}:{n_}"
                        prim[n_]()
                return f
            stg = [mk(g_) for g_ in CFG["sb_stages"]]
            return stg + [None] * (NSTG - len(stg))

        def mla_unit(cm, hd, blk, ib, is_first_blk, knres, krres, vres, fin, last_of_group):
            kcol, w, vb = blk
            c = hd // 2
            pb = 64 * (hd % 2)
            gq = hd // 3
            rb = 32 * (hd % 3)
            hl = hd % 2
            q4 = hd // 2
            t_lo = 0 if ib is None else ib
            N = (4 - t_lo) * 128
            tc0 = t_lo * 128
            st = {}

            def sA():
                if cm["obm"] is None:
                    cm["obm"] = [bank(), bank()]
                    for b_ in cm["obm"]:
                        held.add(b_)
                zb = bank()
                held.add(zb)
                st["zb"] = zb
                mm(ps[0:w, zb, 0:N], kT_nope[:, c, kcol:kcol + w], qm_nope[:, hd, tc0:G], True, False,
                   [("qT_nope",)] + knres, [("ps", zb)])
                mm(ps[0:w, zb, 0:N], kT_rope[:, kcol:kcol + w], qm_rope[:, hd, tc0:G], False, ib is None,
                   [("qT_rope",)] + krres, [("ps", zb)])
                if ib is not None:
                    mm(ps[:, zb, 0:128], ident[:, :], negm2[:, :], False, True, [], [("ps", zb)])

            def sB():
                zb = st["zb"]
                r = rotP[0] % NP
                rotP[0] += 1
                st["r"] = r
                P = pT[r]
                act(P[0:w, 0:N], ps[0:w, zb, 0:N], AF.Exp, [("ps", zb)], [("pT", r)], scale=MLA_SCALE)
                held.discard(zb)

            def sC():
                r = st["r"]
                P = pT[r]
                obm = cm["obm"]
                for i in range(t_lo, 4):
                    off = (i - t_lo) * 128
                    ob = obm[i // 2]
                    oc = (i % 2) * 130 + hl * 65
                    mm(ps[:, ob, oc:oc + 65], P[0:w, off:off + 128], v_mla[0:w, vb, hd, :],
                       is_first_blk and i % 2 == 0, (ib is not None and i == ib), [("pT", r)] + vres, [("ps", ob)], skip=True)
                if fin:
                    for i in range(4):
                        ob = obm[i // 2]
                        base = (i % 2) * 130
                        o3 = ps[:, ob, base:base + 130].rearrange("p (h d) -> p h d", h=2)
                        dve((lambda e, o3=o3, i=i: e.reciprocal(out=rc[:, 2 * i:2 * i + 2].unsqueeze(2), in_=o3[:, :, 64:65])),
                            [("ps", ob)], [("rc", i)])
                        dve((lambda e, o3=o3, i=i: e.tensor_tensor(
                            out=ytok2[:, i * 128:(i + 1) * 128].rearrange("p (h d) -> p h d", h=2), in0=o3[:, :, 0:64],
                            in1=rc[:, 2 * i:2 * i + 2].unsqueeze(2).to_broadcast([128, 2, 64]), op=ALU.mult)),
                            [("ps", ob), ("rc", i)], [("ytok2", i)])
                    if last_of_group:
                        for b_ in obm:
                            held.discard(b_)
                        cm["obm"] = None
                    tb = bank()
                    pv = psbf(tb)
                    for i in range(4):
                        tr(pv[:, i * 128:(i + 1) * 128], ytok2[:, i * 128:(i + 1) * 128], ident[:, :], [("ytok2", i)], [("ps", tb)])
                    act(actT[:, 12 + q4, :], pv[:, 0:512], AF.Copy, [("ps", tb)], [("actT", 12 + q4)])

            prim = {"A": sA, "B": sB, "C": sC}

            def mk(names):
                def f():
                    for n_ in names:
                        Sched.curtag = f"mla_h{hd}b{vb}:{n_}"
                        prim[n_]()
                return f
            stg = [mk(g_) for g_ in CFG["mla_stages"]]
            return stg + [None] * (NSTG - len(stg))

        def stage_attn(g):
            COLD[0] = False
            try:
                _stage_attn(g)
            finally:
                COLD[0] = False

        def _stage_attn(g):
            gs_ = g % GPS
            gl = ["m"] + list(range(gs_ + 1))
            kres = [("kT_sb", q) for q in gl]
            vres = [("v_sb", q) for q in gl]
            knres = [("kT_nope", q) for q in gl]
            krres = [("kT_rope", q) for q in gl]
            vmres = [("v_mla", q) for q in gl]
            su = []
            for i in range(4):
                j = gs_ * 4 + i
                blocks = key_blocks(j)
                chunks = [blocks[0:4]]
                rest = blocks[4:]
                while rest:
                    chunks.append(rest[0:4])
                    rest = rest[4:]
                ctx = {"t0": i * 128, "ob": None, "carry": {}}
                nch = len(chunks)
                for hd in range(8):
                    for q_, ci in enumerate(range(nch - 1, -1, -1)):
                        su.append(sb_unit(ctx, hd, chunks[ci], ci == nch - 1, q_ == 0, q_ == nch - 1, kres, vres,
                                          hd == 7 and q_ == nch - 1))
            allb = key_blocks(gs_ * 4 + 3)
            mu = []
            cm = {"obm": None}
            for hd in range(8):
                for bi_, blk in enumerate(allb):
                    xb = blk[2] - 1
                    ib = xb - gs_ * 4 if xb >= gs_ * 4 else None
                    lastb = bi_ == len(allb) - 1
                    mu.append(mla_unit(cm, hd, blk, ib, bi_ == 0, knres, krres, vmres,
                                       hd % 2 == 1 and lastb, hd == 7 and lastb))
            units = []
            a = b = 0
            while a < len(su) or b < len(mu):
                if b >= len(mu) or (a < len(su) and a * len(mu) <= b * len(su)):
                    units.append(su[a])
                    a += 1
                else:
                    units.append(mu[b])
                    b += 1
            run_pipeline(units)

        def stage_mixout(tiles, tok):
            load_gpost(1)
            for cp in range(4):
                c0 = cp * 256
                so, ko = wload([
                    (lambda s_: v3(s_, 8, 256)[:, 0:4, :], wsbo_d[:, c0:c0 + 256].rearrange("(kc p) n -> p kc n", p=128)),
                    (lambda s_: v3(s_, 8, 256)[:, 4:8, :], wmlao_d[:, c0:c0 + 256].rearrange("(kc p) n -> p kc n", p=128)),
                ])
                WO = v3(so, 8, 256)
                s1, k1 = wload([(lambda s_: v3(s_, 8, 256), win_cols(2208 + c0, 256))])
                s2, k2 = wload([(lambda s_: v3(s_, 8, 256), win_cols(2208 + 1024 + c0, 256))])
                W1 = v3(s1, 8, 256)
                W2 = v3(s2, 8, 256)
                for jj in range(2):
                    c = 2 * cp + jj
                    cs = slice(jj * 128, (jj + 1) * 128)
                    b1 = bank()
                    for kc in range(4):
                        mm(ps[:, b1, 0:tok], WO[:, kc, cs], yT_sb[:, kc, 0:tok], kc == 0, kc == 3, [ko] + K_ysb, [("ps", b1)])
                    b2 = bank()
                    for kc in range(4):
                        mm(ps[:, b2, 0:tok], WO[:, 4 + kc, cs], yT_mla[:, kc, 0:tok], kc == 0, kc == 3, [ko] + K_ymla, [("ps", b2)])
                    b3 = bank()
                    for kc in range(8):
                        mm(ps[:, b3, 0:tok], W1[:, kc, cs], uT[:, kc, 0:tok], kc == 0, kc == 7, [k1, ("uT",)], [("ps", b3)])
                    b4 = bank()
                    for kc in range(8):
                        mm(ps[:, b4, 0:tok], W2[:, kc, cs], uT[:, kc, 0:tok], kc == 0, kc == 7, [k2, ("uT",)], [("ps", b4)])
                    act(gs[:, 0:tok], ps[:, b3, 0:tok], AF.Sigmoid, [("ps", b3)], [("pT", 0)], bias=bg[:, c:c + 1])
                    act(gm[:, 0:tok], ps[:, b4, 0:tok], AF.Sigmoid, [("ps", b4)], [("pT", 1)], bias=bg[:, 8 + c:9 + c])
                    dve((lambda e, b1=b1: e.tensor_tensor(out=mt1[:, 0:tok], in0=ps[:, b1, 0:tok], in1=gs[:, 0:tok], op=ALU.mult)),
                        [("ps", b1), ("pT", 0)], [("E", 0)])
                    dve((lambda e, b2=b2: e.tensor_tensor(out=mt2[:, 0:tok], in0=ps[:, b2, 0:tok], in1=gm[:, 0:tok], op=ALU.mult)),
                        [("ps", b2), ("pT", 1)], [("E", 1)])
                    dve((lambda e, c=c: e.tensor_tensor(out=mixT[:, c, 0:tok], in0=mt1[:, 0:tok], in1=mt2[:, 0:tok], op=ALU.add)),
                        [("E", 0), ("E", 1)], [("actT", c)])
            ws = []
            for q in range(4):
                st, kk = wload([(lambda s_: v3(s_, 2, D), wout_d[q * 256:(q + 1) * 256, :].rearrange("(j p) n -> p j n", p=128))])
                ws.append((v3(st, 2, D), kk))
            bank_ctr[0] = 0
            for i, (t0, r) in enumerate(tiles):
                for nh in range(2):
                    b = 2 * i + nh
                    for kc in range(8):
                        W, kk = ws[kc // 2]
                        mm(ps[0:r, b, :], mixT[:, kc, t0:t0 + r], W[:, kc % 2, nh * 512:(nh + 1) * 512], kc == 0, kc == 7,
                           [kk, ("actT", kc)], [("ps", b)])
            stage_post(tiles, 1, half=False)
            bank_ctr[0] = 2 * len(tiles)

        def stage_store(g):
            ops_ = []
            for i in range(4):
                r0 = g * G + i * 128
                ops_.append(S.add("sp", (lambda e, i=i, r0=r0: e.dma_start(out=out_d[r0:r0 + 128, :], in_=h[:, i, :])),
                                  reads=[("h", i)], writes=[("out", g, i)], dma=f"ho{i}", cost=(100.0, 524288.0)))
            return ops_

        MT = [(0, NMETA)]
        GT = [(i * 128, 128) for i in range(4)]
        prog = []
        prog += [("load", -1), ("rope", -1), ("norm", MT, 0), ("ffn", MT, 0, NMETA), ("norm", MT, 1), ("proj", MT, NMETA, -1)]
        for g in range(NG):
            prog += [("load", g), ("rope", g), ("norm", GT, 0), ("ffn", GT, 0, G), ("norm", GT, 1), ("proj", GT, G, g),
                     ("attn", g), ("mixout", GT, G), ("norm", GT, 2), ("ffn", GT, 1, G), ("store", g)]
        if dbg is not None and "simattn" in dbg:
            COLD[0] = False
            stage_attn(dbg["simattn"])
            res = S.simulate()
            if dbg.get("crit"):
                allops = [o for e_ in S.ENGS for o in S.ops[e_]]
                last = max(allops, key=lambda o: S.sim_done[id(o)])
                import collections
                cat = collections.Counter()
                op = last
                while op is not None:
                    kind, p = S.sim_pred[id(op)]
                    cat[(op.eng, kind)] += 1
                    dur = S.sim_done[id(op)] - S.sim_start[id(op)]
                    cat[("time", op.eng)] += dur
                    op = p
                print(sorted(cat.items(), key=lambda kv: -kv[1])[:14])
                op = last
                path = []
                while op is not None and len(path) < 400:
                    kind, p = S.sim_pred[id(op)]
                    path.append(f"{S.sim_start[id(op)]/1e3:8.2f} {op.eng:4s} {op.tag:22s} {kind} dur={S.sim_done[id(op)]-S.sim_start[id(op)]:.0f}")
                    op = p
                for l_ in path[200:260][::-1]:
                    print(l_)
            return res
        if dbg is not None:
            prog = prog[:dbg["nstage"]]
        fns = {"load": stage_load, "rope": stage_rope_tab, "norm": stage_norm, "ffn": stage_ffn, "proj": stage_proj,
               "attn": stage_attn, "mixout": stage_mixout, "store": stage_store}
        stored = []
        for st_ in prog:
            fns[st_[0]](*st_[1:])
            if st_[0] == "store":
                stored.append(st_[1])
        fin = S.add("sp", None, reads=[("out", g, i) for g in stored for i in range(4)])
        if dbg is not None and dbg.get("simall"):
            res = S.simulate()
            return res[0], res[1], S.dma_busy
        S.finalize()

        esem = {e_: es.enter_context(nc.semaphore(f"es_{e_}")) for e_ in Sched.ENGS}
        dsem = {n_: es.enter_context(nc.semaphore(f"ds_{n_}")) for n_ in S.dmacnt}
        with nc.Block() as block:
            @block.tensor
            def _(e):
                S.run(e, "pe", esem, dsem)

            @block.scalar
            def _(e):
                S.run(e, "act", esem, dsem)

            @block.vector
            def _(e):
                S.run(e, "dve", esem, dsem)

            @block.gpsimd
            def _(e):
                S.run(e, "pool", esem, dsem)

            @block.sync
            def _(e):
                S.run(e, "sp", esem, dsem)
        if dbg is not None and dbg.get("dump"):
            T = dict(h=h, uT=uT, actT=actT, kT_sb=kT_sb, v_sb=v_sb, kT_nope=kT_nope, kT_rope=kT_rope, v_mla=v_mla,
                     qT_sb=qm_sb, qT_nope=qm_nope, qT_rope=qm_rope, xn=xn, stat=stat,
                     gpre=gpre, bg=bg, cosT=cosT, sinT=sinT, EB=EB, ytok=ytok)
            dsm = es.enter_context(nc.semaphore("dumpsem"))
            with nc.Block() as block:
                @block.gpsimd
                def _(e):
                    n = 0
                    for name in dbg["dump"]:
                        t = T[name]
                        shp = list(t.shape)
                        dd = nc.dram_tensor("dbg_" + name, shp, F32, kind="ExternalOutput").ap()
                        e.dma_start(out=dd, in_=t[:]).then_inc(dsm, 16)
                        n += 16
                    e.wait_ge(dsm, n)
    return nc


def _rope_table():
    half = 16
    inv = (10000.0 ** (-np.arange(half, dtype=np.float32) / half)).astype(np.float32)
    pos = np.arange(KLEN, dtype=np.float32)
    ang = pos[None, :] * inv[:, None]
    cos = np.cos(ang).astype(np.float32)
    sin = np.sin(ang).astype(np.float32)
    c32 = np.concatenate([cos, cos], axis=0)
    s32 = np.concatenate([-sin, sin], axis=0)
    tab = np.stack([np.tile(c32, (3, 1)), np.tile(s32, (3, 1))], axis=0)
    return np.ascontiguousarray(tab, dtype=np.float32)


def kernel(**inputs):
    f = lambda a: np.ascontiguousarray(np.asarray(a), dtype=np.float32)
    x = f(inputs["x"])
    shared = {
        "meta": f(inputs["meta_tokens"]),
        "gains_pre": np.ascontiguousarray(np.concatenate(
            [f(inputs["ffn1_pre_g"]), f(inputs["mix_pre_g"]), f(inputs["ffn2_pre_g"])], axis=0)),
        "gains_post": np.ascontiguousarray(np.concatenate(
            [f(inputs["ffn1_post_g"]), f(inputs["mix_post_g"]), f(inputs["ffn2_post_g"])], axis=1)),
        "ffn1_w_gate": f(inputs["ffn1_w_gate"])[0], "ffn2_w_gate": f(inputs["ffn2_w_gate"])[0],
        "ffn1_w_up": f(inputs["ffn1_w_up"])[0], "ffn2_w_up": f(inputs["ffn2_w_up"])[0],
        "ffn1_w_down": f(inputs["ffn1_w_down"])[0], "ffn2_w_down": f(inputs["ffn2_w_down"])[0],
        "w_in": f(inputs["w_in"])[0],
        "b_gate": f(inputs["b_gate"]),
        "q_norm_g": f(inputs["q_norm_g"]),
        "kv_norm_g": f(inputs["kv_norm_g"]),
        "w_uq": f(inputs["w_uq"])[0],
        "w_ukv": f(inputs["w_ukv"])[0],
        "w_sb_o": f(inputs["w_sb_o"])[0],
        "w_mla_o": f(inputs["w_mla_o"])[0],
        "w_out": f(inputs["w_out"])[0],
        "rope_tab": _rope_table(),
    }
    nc = build_nc()
    in_maps = []
    for c in range(NCORES):
        m = dict(shared)
        m["x"] = np.ascontiguousarray(x[c * SEQ_PER_CORE:(c + 1) * SEQ_PER_CORE].reshape(XTOK, D))
        in_maps.append(m)
    res = run_bass_kernel_spmd(nc, in_maps, core_ids=list(range(NCORES)))
    out = np.stack([np.asarray(r["out"], dtype=np.float32).reshape(SEQ_PER_CORE, SEQ, D) for r in res.results], axis=0)
    return out.reshape(NCORES * SEQ_PER_CORE, SEQ, D)
```

```python
import numpy as np
from contextlib import ExitStack
import concourse.bass as bass
import concourse.mybir as mybir
from concourse.bass_utils import run_bass_kernel_spmd

F32 = mybir.dt.float32
BF16 = mybir.dt.bfloat16
AF = mybir.ActivationFunctionType
ALU = mybir.AluOpType

NCORES = 8
D = 1024
DFF = 2816
NFF = DFF // 128
SEQ = 2048
NMETA = 16
SEQ_PER_CORE = 2
XTOK = SEQ_PER_CORE * SEQ
G = 512
NG = XTOK // G
GPS = SEQ // G
KLEN = NMETA + SEQ
NVB = 1 + SEQ // 128
IN_COLS = 4256
EPS = 1e-6
MLA_SCALE = 96.0 ** -0.5
NSLOT = 6
CFG = {
    "sb_stages": [["Z"], ["A"], [], ["A2"], ["B"], [], [], ["C"], ["D"], ["E"]],
    "mla_stages": [["A"], ["B"], [], ["C"]],
    "order": "rev",
    "NE": 3, "NSP": 3, "NS": 2, "NA": 3, "NT": 3, "NP": 3,
}
SLOT_ELEMS = 2048


class Op:
    __slots__ = ("eng", "fn", "deps", "idx", "signal", "dma", "sigcount", "waits", "cost", "seq", "tag")


class Sched:
    ENGS = ("pe", "act", "dve", "pool", "sp")
    curtag = ""

    def __init__(self):
        self.ops = {e: [] for e in self.ENGS}
        self.lastw = {}
        self.readers = {}
        self.dmacnt = {}
        self.nseq = 0

    def simulate(self, semlat=120.0):
        allops = sorted((o for e in self.ENGS for o in self.ops[e]), key=lambda o: o.seq)
        t_eng = {e: 0.0 for e in self.ENGS}
        busy = {e: 0.0 for e in self.ENGS}
        done = {}
        self.dma_free = 0.0
        self.dma_busy = 0.0
        pred = {}
        starts = {}
        last_on = {}
        for op in allops:
            ready = 0.0
            rd = None
            for d in op.deps:
                if d.eng == "pe" and op.eng == "pe":
                    continue
                if done[id(d)] + semlat > ready:
                    ready = done[id(d)] + semlat
                    rd = d
            start = max(t_eng[op.eng], ready)
            if op.dma is not None:
                occ, nbytes = op.cost
                t_eng[op.eng] = start + occ
                busy[op.eng] += occ
                xs = max(start + occ, self.dma_free)
                self.dma_free = xs + nbytes / 300.0
                self.dma_busy += nbytes / 300.0
                done[id(op)] = self.dma_free + 2000.0
                pred[id(op)] = ("dep", rd) if ready > start - 1e-9 else ("eng", last_on.get(op.eng))
                last_on[op.eng] = op
                starts[id(op)] = start
                continue
            if ready > t_eng[op.eng]:
                pred[id(op)] = ("dep", rd)
            else:
                pred[id(op)] = ("eng", last_on.get(op.eng))
            last_on[op.eng] = op
            starts[id(op)] = start
            occ, lat = op.cost
            t_eng[op.eng] = start + occ
            busy[op.eng] += occ
            done[id(op)] = start + occ + lat
        self.sim_done = done
        self.sim_pred = pred
        self.sim_start = starts
        return max(done.values()), busy

    def add(self, eng, fn, reads=(), writes=(), dma=None, nodep=(), cost=None):
        op = Op()
        op.cost = cost if cost is not None else (600.0, 0.0)
        op.seq = self.nseq
        self.nseq += 1
        op.tag = Sched.curtag
        op.eng = eng
        op.fn = fn
        op.signal = False
        op.sigcount = 0
        op.dma = None
        deps = {}
        for r in reads:
            w = self.lastw.get(r)
            if w is not None:
                deps[id(w)] = w
            if r[0] == "ps":
                for rd in self.readers.get(r, ()):
                    if rd.eng != eng:
                        deps[id(rd)] = rd
        for r in writes:
            w = self.lastw.get(r)
            if w is not None:
                deps[id(w)] = w
            for rd in self.readers.get(r, ()):
                deps[id(rd)] = rd
        for r in reads:
            self.readers.setdefault(r, []).append(op)
        for r in writes:
            self.lastw[r] = op
            self.readers[r] = []
        for n in nodep:
            deps.pop(id(n), None)
        deps.pop(id(op), None)
        op.deps = list(deps.values())
        if dma is not None:
            c = self.dmacnt.get(dma, 0) + 16
            self.dmacnt[dma] = c
            op.dma = (dma, c)
        op.idx = len(self.ops[eng])
        self.ops[eng].append(op)
        return op

    def finalize(self):
        for eng in self.ENGS:
            wm = {}
            dwm = {}
            for op in self.ops[eng]:
                need = {}
                dwaits = []
                for d in op.deps:
                    if d.dma is not None:
                        name, val = d.dma
                        if dwm.get(name, 0) < val:
                            dwm[name] = val
                            dwaits.append((name, val))
                    else:
                        if d.eng == "pe" and eng == "pe":
                            continue
                        if wm.get(d.eng, -1) < d.idx:
                            if d.eng not in need or need[d.eng].idx < d.idx:
                                need[d.eng] = d
                for e2, d in need.items():
                    wm[e2] = d.idx
                    d.signal = True
                op.waits = (list(need.values()), dwaits)
        for eng in self.ENGS:
            cnt = 0
            for op in self.ops[eng]:
                if op.signal and op.dma is None:
                    cnt += 1
                    op.sigcount = cnt

    def run(self, e, eng, esem, dsem):
        for op in self.ops[eng]:
            for d in op.waits[0]:
                e.wait_ge(esem[d.eng], d.sigcount)
            for (name, val) in op.waits[1]:
                e.wait_ge(dsem[name], val)
            if op.fn is None:
                continue
            ins = op.fn(e)
            if op.dma is not None:
                ins.then_inc(dsem[op.dma[0]], 16)
            elif op.signal:
                ins.then_inc(esem[eng], 1)


def build_nc(dbg=None):
    nc = bass.Bass("TRN2", target_bir_lowering=False)

    def din(name, shape):
        return nc.dram_tensor(name, list(shape), F32, kind="ExternalInput").ap()

    x_d = din("x", [XTOK, D])
    meta_d = din("meta", [NMETA, D])
    gains_pre_d = din("gains_pre", [3, D])
    gains_post_d = din("gains_post", [1, 3 * D])
    wg_d = [din("ffn1_w_gate", [D, DFF]), din("ffn2_w_gate", [D, DFF])]
    wu_d = [din("ffn1_w_up", [D, DFF]), din("ffn2_w_up", [D, DFF])]
    wd_d = [din("ffn1_w_down", [DFF, D]), din("ffn2_w_down", [DFF, D])]
    win_d = din("w_in", [D, IN_COLS])
    bgate_d = din("b_gate", [1, 2 * D])
    qg_d = din("q_norm_g", [1, 384])
    kvg_d = din("kv_norm_g", [1, 256])
    wuq_d = din("w_uq", [384, 8, 96])
    wukv_d = din("w_ukv", [256, 8, 128])
    wsbo_d = din("w_sb_o", [512, D])
    wmlao_d = din("w_mla_o", [512, D])
    wout_d = din("w_out", [D, D])
    rope_d = din("rope_tab", [2, 96, KLEN])
    out_d = nc.dram_tensor("out", [XTOK, D], F32, kind="ExternalOutput").ap()

    S = Sched()
    es = ExitStack()
    with es:
        def sb(name, shape, dt=F32):
            return es.enter_context(nc.sbuf_tensor(name, list(shape), dt))

        ident = sb("ident", [128, 128], BF16)
        ones_bf = sb("ones_bf", [128, 128], BF16)
        ones_f = sb("ones_f", [128, 512], F32)
        maskc = sb("maskc", [128, 128], F32)
        negm = sb("negm", [128, 128], BF16)
        negm2 = sb("negm2", [128, 128], BF16)
        gpre = sb("gpre", [128, 3, 8], F32)
        gpost1 = sb("gpost1", [128, D], F32)
        bg = sb("bg", [128, 16], F32)
        qg = sb("qg", [128, 3], F32)
        kvg = sb("kvg", [128, 2], F32)
        cosT = sb("cosT", [96, G], F32)
        sinT = sb("sinT", [96, G], F32)
        h = sb("h", [128, 4, D], F32)
        xn = sb("xn", [128, D], BF16)
        xn2 = [xn, xn]
        stat = sb("stat", [128, 32], F32)
        uT = sb("uT", [128, 8, G], BF16)
        actT = sb("actT", [128, NFF, G], BF16)
        slots = [sb(f"slot{i}", [128, SLOT_ELEMS], BF16) for i in range(NSLOT)]
        qm_sb = sb("qm_sb", [128, 8, G], BF16)
        kT_sb = sb("kT_sb", [128, 4, KLEN], BF16)
        v_sb = sb("v_sb", [128, NVB, 512], BF16)
        qm_nope = sb("qm_nope", [128, 8, G], BF16)
        qm_rope = sb("qm_rope", [96, 8, G], BF16)
        kT_nope = sb("kT_nope", [128, 4, KLEN], BF16)
        kT_rope = sb("kT_rope", [96, KLEN], BF16)
        v_mla = sb("v_mla", [128, NVB, 8, 65], BF16)
        NE, NS, NA, NT_, NP, NSP = CFG["NE"], CFG["NS"], CFG["NA"], CFG["NT"], CFG["NP"], CFG["NSP"]
        EB = sb("EB", [128, NE, 512], F32)
        Ebuf = [EB[:, i, :] for i in range(NE)]
        ptmp = EB[:, 0:2, :].rearrange("p a b -> p (a b)")
        identf = EB[:, 0, 0:128]
        mt1 = Ebuf[0]
        mt2 = Ebuf[1]
        Sfx = [sb(f"Sfx{i}", [128, 512], F32) for i in range(NS)]
        SPB = sb("SPB", [128, NSP, 512], F32)
        SPbuf = [SPB[:, i, :] for i in range(NSP)]
        ptmp2 = SPB[:, 0:2, :].rearrange("p a b -> p (a b)")
        abuf = [sb(f"abuf{i}", [128, 512], BF16) for i in range(NA)]
        aT = [sb(f"aT{i}", [128, 4, 128], BF16) for i in range(NT_)]
        pT = [sb(f"pT{i}", [128, 512], BF16) for i in range(NP)]
        ytok = sb("ytok", [128, 512], BF16)
        ytok2 = sb("ytok2", [128, 512], BF16)
        rc = sb("rc", [128, 8], F32)
        mixT = actT[:, 0:8, :]
        yT_sb = actT[:, 8:12, :]
        yT_mla = actT[:, 12:16, :]
        cqT = actT[:, 16:19, :]
        ckvT = actT[:, 19:21, :]
        K_ysb = [("actT", k) for k in range(8, 12)]
        K_ymla = [("actT", k) for k in range(12, 16)]
        K_cq = [("actT", k) for k in range(16, 19)]
        K_ckv = [("actT", k) for k in range(19, 21)]
        rt1 = Sfx[0][0:96, :]
        rt2 = Sfx[1][0:96, :]
        sg = [abuf[0], abuf[1]]
        gs = pT[0]
        gm = pT[1]
        rq_rep = Ebuf[0]
        rkv_rep = Ebuf[1]
        ps = es.enter_context(nc.psum_tensor("ps", [128, 8, 512], F32))

        csem = es.enter_context(nc.semaphore("csem"))
        psem = es.enter_context(nc.semaphore("psem"))
        with nc.Block() as block:
            @block.sync
            def _(e):
                n = 0
                def col(dst, src_row, c):
                    return e.dma_start(out=dst, in_=src_row[:, c * 128:(c + 1) * 128].rearrange("a p -> p a"),
                                       allow_slow_non_contiguous=True)
                for a in range(3):
                    for c in range(8):
                        col(gpre[:, a, c:c + 1], gains_pre_d[a:a + 1, :], c).then_inc(csem, 16); n += 16
                for c in range(16):
                    col(bg[:, c:c + 1], bgate_d, c).then_inc(csem, 16); n += 16
                for c in range(3):
                    col(qg[:, c:c + 1], qg_d, c).then_inc(csem, 16); n += 16
                for c in range(2):
                    col(kvg[:, c:c + 1], kvg_d, c).then_inc(csem, 16); n += 16
                e.wait_ge(csem, n)

            @block.gpsimd
            def _(e):
                e.memset(identf, 1.0).then_inc(psem, 1)
                e.memset(maskc[:], 1.0).then_inc(psem, 1)
                e.wait_ge(psem, 2)
                e.affine_select(out=identf, in_=identf, pattern=[[-1, 128]],
                                compare_op=ALU.is_equal, fill=0.0, base=0, channel_multiplier=1).then_inc(psem, 1)
                e.affine_select(out=maskc[:], in_=maskc[:], pattern=[[-1, 128]],
                                compare_op=ALU.is_ge, fill=0.0, base=-1, channel_multiplier=1).then_inc(psem, 1)
                e.wait_ge(psem, 4)
                e.tensor_scalar(out=negm[:], in0=maskc[:], scalar1=-1.0, scalar2=30000.0, op0=ALU.add, op1=ALU.mult)
                e.memset(negm2[:], 0.0).then_inc(psem, 1)
                e.wait_ge(psem, 5)
                e.memset(negm2[64:128, 0:64], -30000.0)
                e.tensor_copy(out=ident[:], in_=identf)
                e.memset(ones_bf[:], 1.0)
                e.memset(ones_f[:], 1.0)
                e.memset(v_mla[:].rearrange("p a b c -> p (a b c)"), 1.0)
                e.memset(h[:].rearrange("p a d -> p (a d)"), 0.0)
                e.memset(qm_sb[:].rearrange("p a d -> p (a d)"), 0.0)
                e.memset(qm_nope[:].rearrange("p a d -> p (a d)"), 0.0)
                e.memset(qm_rope[:].rearrange("p a d -> p (a d)"), 0.0)

        bank_ctr = [0]

        held = set()

        def bank():
            for _ in range(16):
                b = bank_ctr[0] % 8
                bank_ctr[0] += 1
                if b not in held:
                    return b
            raise RuntimeError("all PSUM banks held")

        def bank_pair():
            while True:
                if bank_ctr[0] % 2:
                    bank_ctr[0] += 1
                b = bank_ctr[0] % 8
                bank_ctr[0] += 2
                if b not in held and (b + 1) not in held:
                    return b

        slot_ctr = [0]

        def wload(pieces):
            k = slot_ctr[0] % NSLOT
            slot_ctr[0] += 1
            st = slots[k]
            prev = []
            for (dfn, src) in pieces:
                dst = dfn(st)
                op = S.add("pool", (lambda e, dst=dst, src=src: e.dma_start(out=dst, in_=src)),
                           writes=[("slot", k)], dma=f"slot{k}", nodep=prev, cost=(1000.0, dst.size() * 4.0))
                prev.append(op)
            return st, ("slot", k)

        def v3(st, a, b):
            return st[:, 0:a * b].rearrange("p (a b) -> p a b", a=a)

        COLD = [False]

        def pe_cost(n):
            return (max(64.0, n / 1.2), 110.0) if COLD[0] else (max(40.0, n / 2.4 + 12), 60.0)

        def mm(out, lhsT, rhs, start, stop, reads, writes, skip=False):
            cst = pe_cost(out.free_size())
            if skip:
                S.add("pe", (lambda e: e.matmul(out, lhsT=lhsT, rhs=rhs, start=start, stop=stop, skip_group_check=True)), reads, writes, cost=cst)
            else:
                S.add("pe", (lambda e: e.matmul(out, lhsT=lhsT, rhs=rhs, start=start, stop=stop)), reads, writes, cost=cst)

        def tr(out, in_, idn, reads, writes):
            S.add("pe", (lambda e: e.transpose(out, in_, idn)), reads, writes, cost=(107.0, 45.0) if COLD[0] else (56.0, 45.0))

        def act(out, in_, func, reads, writes, **kw):
            cst = (224.0 + 0.7 * in_.free_size(), 0.0)
            if globals().get("SIM_FREE_LN") and (func == AF.Ln or kw.get("scale") == -1.0):
                cst = (1.0, 0.0)
            S.add("act", (lambda e: e.activation(out=out, in_=in_, func=func, **kw)), reads, writes, cost=cst)

        def dve(fn, reads, writes, cost=None):
            S.add("dve", fn, reads, writes, cost=cost)

        def psbf(b):
            return ps[:, b, :].bitcast(BF16)

        def stage_load(g):
            if g < 0:
                S.add("sp", (lambda e: e.dma_start(out=h[0:NMETA, 0, :], in_=meta_d[:, :])),
                      writes=[("h", 0)], dma="hl0")
                return
            for i in range(4):
                r0 = g * G + i * 128
                S.add("sp", (lambda e, i=i, r0=r0: e.dma_start(out=h[:, i, :], in_=x_d[r0:r0 + 128, :])),
                      writes=[("h", i)], dma=f"hl{i}", cost=(100.0, 524288.0))

        def stage_rope_tab(g):
            kp = 0 if g < 0 else NMETA + (g % GPS) * G
            n = NMETA if g < 0 else G
            S.add("sp", (lambda e: e.dma_start(out=cosT[:, 0:n], in_=rope_d[0, :, kp:kp + n])),
                  writes=[("cosT",)], dma="ropec")
            S.add("sp", (lambda e: e.dma_start(out=sinT[:, 0:n], in_=rope_d[1, :, kp:kp + n])),
                  writes=[("sinT",)], dma="ropes")

        def stage_norm(tiles, gi):
            for i, (t0, r) in enumerate(tiles):
                junk = uT[0:r, 2 * i:2 * i + 2, :].rearrange("p a b -> p (a b)")
                act(junk, h[0:r, i, :], AF.Square, [("h", i)], [("uT",), ("st", 8 + i)], accum_out=stat[0:r, 8 + i:9 + i])
            for i, (t0, r) in enumerate(tiles):
                act(stat[0:r, 12 + i:13 + i], stat[0:r, 8 + i:9 + i], AF.Sqrt, [("st", 8 + i)], [("st", 12 + i)], scale=1.0 / D, bias=EPS)
            for i, (t0, r) in enumerate(tiles):
                rs = stat[0:r, 12 + i:13 + i]
                dve((lambda e, rs=rs: e.reciprocal(out=rs, in_=rs)), [("st", 12 + i)], [("st", 12 + i)])
            for i, (t0, r) in enumerate(tiles):
                rs = stat[0:r, 12 + i:13 + i]
                xb_ = xn2[i % 2]
                act(xb_[0:r, :], h[0:r, i, :], AF.Copy, [("h", i), ("st", 12 + i)], [("xn", 0)], scale=rs)
                b = bank()
                pv = psbf(b)
                for kc in range(8):
                    tr(pv[:, kc * 128:kc * 128 + r], xb_[0:r, kc * 128:(kc + 1) * 128], ident[0:r, 0:r],
                       [("xn", 0)], [("ps", b)])
                src = pv.rearrange("p (c t) -> p c t", c=8)[:, :, 0:r]
                dst = uT[:, :, t0:t0 + r]
                gv = gpre[:, gi, :].unsqueeze(2).to_broadcast([128, 8, r])
                dve((lambda e, dst=dst, src=src, gv=gv: e.tensor_tensor(out=dst, in0=src, in1=gv, op=ALU.mult)),
                    [("ps", b)], [("uT",)])

        def stage_ffn(tiles, f, tok):
            wg, wu, wd = wg_d[f], wu_d[f], wd_d[f]
            load_gpost(0 if f == 0 else 2)
            for blk in range(NFF // 2):
                c0 = blk * 256
                sa, ka = wload([(lambda st: v3(st, 8, 256), wg[:, c0:c0 + 256].rearrange("(kc p) n -> p kc n", p=128))])
                sbv, kb = wload([(lambda st: v3(st, 8, 256), wu[:, c0:c0 + 256].rearrange("(kc p) n -> p kc n", p=128))])
                A = v3(sa, 8, 256)
                B = v3(sbv, 8, 256)
                for j in range(2):
                    ffc = 2 * blk + j
                    bgt = bank()
                    for kc in range(8):
                        mm(ps[:, bgt, 0:tok], A[:, kc, j * 128:(j + 1) * 128], uT[:, kc, 0:tok], kc == 0, kc == 7,
                           [ka, ("uT",)], [("ps", bgt)])
                    bu = bank()
                    for kc in range(8):
                        mm(ps[:, bu, 0:tok], B[:, kc, j * 128:(j + 1) * 128], uT[:, kc, 0:tok], kc == 0, kc == 7,
                           [kb, ("uT",)], [("ps", bu)])
                    sgi = ffc % 2
                    act(sg[sgi][:, 0:tok], ps[:, bgt, 0:tok], AF.Silu, [("ps", bgt)], [("a", sgi)])
                    dve((lambda e, o=actT[:, ffc, 0:tok], a=ps[:, bu, 0:tok], b2=sg[sgi][:, 0:tok]:
                         e.tensor_tensor(out=o, in0=a, in1=b2, op=ALU.mult)),
                        [("ps", bu), ("a", sgi)], [("actT", ffc)])
            bank_ctr[0] = 0
            for blk in range(NFF // 2):
                r0 = blk * 256
                sw, kw = wload([(lambda st: v3(st, 2, D), wd[r0:r0 + 256, :].rearrange("(j p) n -> p j n", p=128))])
                W = v3(sw, 2, D)
                for j in range(2):
                    ffc = 2 * blk + j
                    for i, (t0, r) in enumerate(tiles):
                        for nh in range(2):
                            b = 2 * i + nh
                            mm(ps[0:r, b, :], actT[:, ffc, t0:t0 + r], W[:, j, nh * 512:(nh + 1) * 512],
                               ffc == 0, ffc == NFF - 1, [kw, ("actT", ffc)], [("ps", b)])
            stage_post(tiles, 0 if f == 0 else 2, half=True)
            bank_ctr[0] = 2 * len(tiles)

        def load_gpost(gi):
            S.add("sp", (lambda e: e.dma_start(out=gpost1[:, :], in_=gains_post_d[0:1, gi * D:(gi + 1) * D].partition_broadcast(128))),
                  writes=[("gpost",)], dma="gp", cost=(100.0, 524288.0))

        def stage_post(tiles, gi, half):
            k = 4.0 if half else 1.0
            for i, (t0, r) in enumerate(tiles):
                fps = ps[0:r, 2 * i:2 * i + 2, :]
                junk = actT[0:r, 2 * i:2 * i + 2, :].rearrange("p a b -> p (a b)")
                act(junk, fps, AF.Square, [("ps", 2 * i), ("ps", 2 * i + 1)],
                    [("actT", 2 * i), ("actT", 2 * i + 1), ("st", 16 + i)], accum_out=stat[0:r, 16 + i:17 + i])
            for i, (t0, r) in enumerate(tiles):
                act(stat[0:r, 20 + i:21 + i], stat[0:r, 16 + i:17 + i], AF.Sqrt, [("st", 16 + i)], [("st", 20 + i)],
                    scale=k / D, bias=k * EPS)
            for i, (t0, r) in enumerate(tiles):
                rs = stat[0:r, 20 + i:21 + i]
                dve((lambda e, rs=rs: e.reciprocal(out=rs, in_=rs)), [("st", 20 + i)], [("st", 20 + i)])
            for i, (t0, r) in enumerate(tiles):
                fps = ps[0:r, 2 * i:2 * i + 2, :]
                rs = stat[0:r, 20 + i:21 + i]
                pt, pk = (ptmp, [("E", 0), ("E", 1)]) if i % 2 == 0 else (ptmp2, [("SP", 0), ("SP", 1)])
                dve((lambda e, fps=fps, rs=rs, r=r, pt=pt: e.scalar_tensor_tensor(
                    out=pt[0:r, :], in0=fps, scalar=rs, in1=gpost1[0:r, :], op0=ALU.mult, op1=ALU.mult)),
                    [("ps", 2 * i), ("ps", 2 * i + 1), ("st", 20 + i), ("gpost",)], pk, cost=(1300.0, 0.0))
                dve((lambda e, i=i, r=r, pt=pt: e.tensor_tensor(out=h[0:r, i, :], in0=h[0:r, i, :], in1=pt[0:r, :], op=ALU.add)),
                    pk + [("h", i)], [("h", i)], cost=(1000.0, 0.0))

        def win_cols(c0, n):
            return win_d[:, c0:c0 + n].rearrange("(kc p) n -> p kc n", p=128)

        def stage_proj(tiles, tok, g):
            is_meta = g < 0
            gi = "m" if is_meta else g % GPS
            kp = 0 if is_meta else NMETA + (g % GPS) * G
            U = [("uT",)]
            PP = 99 if (dbg is None or is_meta) else dbg.get("pp", 99)
            for which in ([1] if is_meta else [0, 1]):
                for half in range(2):
                    c0 = which * 512 + half * 256
                    st, kk = wload([(lambda s_: v3(s_, 8, 256), win_cols(c0, 256))])
                    W = v3(st, 8, 256)
                    for j in range(2):
                        c = half * 2 + j
                        b = bank()
                        for kc in range(8):
                            mm(ps[:, b, 0:tok], W[:, kc, j * 128:(j + 1) * 128], uT[:, kc, 0:tok], kc == 0, kc == 7,
                               [kk] + U, [("ps", b)])
                        if which == 0:
                            act(qm_sb[0:64, 2 * c, 0:tok], ps[0:64, b, 0:tok], AF.Copy, [("ps", b)], [("qT_sb",)], scale=0.125)
                            act(qm_sb[64:128, 2 * c + 1, 0:tok], ps[64:128, b, 0:tok], AF.Copy, [("ps", b)], [("qT_sb",)], scale=0.125)
                        else:
                            act(kT_sb[:, c, kp:kp + tok], ps[:, b, 0:tok], AF.Copy, [("ps", b)], [("kT_sb", gi)])
            if PP <= 1:
                return
            sv = []
            for half in range(2):
                st, kk = wload([(lambda s_: v3(s_, 8, 256), win_cols(1024 + half * 256, 256))])
                sv.append((v3(st, 8, 256), kk))
            for i, (t0, r) in enumerate(tiles):
                b = bank()
                for half in range(2):
                    W, kk = sv[half]
                    for kc in range(8):
                        mm(ps[0:r, b, half * 256:(half + 1) * 256], uT[:, kc, t0:t0 + r], W[:, kc, :], kc == 0, kc == 7,
                           [kk] + U, [("ps", b)])
                vb = 0 if is_meta else 1 + (g % GPS) * 4 + i
                dve((lambda e, o=v_sb[0:r, vb, :], a=ps[0:r, b, :]: e.tensor_copy(out=o, in_=a)),
                    [("ps", b)], [("v_sb", gi)])
            if PP <= 2:
                return
            if not is_meta:
                st1, k1 = wload([(lambda s_: v3(s_, 8, 256), win_cols(1536, 256))])
                st2, k2 = wload([(lambda s_: v3(s_, 8, 256), win_cols(1792, 256))])
                lw = [(v3(st1, 8, 256), 0, k1), (v3(st1, 8, 256), 128, k1), (v3(st2, 8, 256), 0, k2)]
                for c in range(3):
                    W, off, kk = lw[c]
                    b = bank()
                    for kc in range(8):
                        mm(ps[:, b, 0:tok], W[:, kc, off:off + 128], uT[:, kc, 0:tok], kc == 0, kc == 7, [kk] + U, [("ps", b)])
                    dve((lambda e, o=cqT[:, c, 0:tok], a=ps[:, b, 0:tok], s=qg[:, c:c + 1]:
                         e.tensor_scalar(out=o, in0=a, scalar1=s, scalar2=None, op0=ALU.mult)),
                        [("ps", b)], K_cq + [("ser",)])
                    act(pT[c][:, 0:tok], ps[:, b, 0:tok], AF.Square, [("ps", b)], [("pT", c), ("ser",)])
                if PP == 3 and dbg.get("sub", 0) == 1:
                    return
                b = bank()
                for c in range(3):
                    mm(ps[:, b, 0:tok], ones_bf[:, :], pT[c][:, 0:tok], c == 0, c == 2, [("pT", c)], [("ps", b)])
                act(rq_rep[:, 0:tok], ps[:, b, 0:tok], AF.Sqrt, [("ps", b)], [("E", 0)], scale=1.0 / 384, bias=EPS)
                dve((lambda e: e.reciprocal(out=rq_rep[:, 0:tok], in_=rq_rep[:, 0:tok])), [("E", 0)], [("E", 0)])
            if PP <= 3:
                return
            st1, k1 = wload([(lambda s_: v3(s_, 8, 256), win_cols(1920, 256))])
            W = v3(st1, 8, 256)
            for c in range(2):
                b = bank()
                for kc in range(8):
                    mm(ps[:, b, 0:tok], W[:, kc, c * 128:(c + 1) * 128], uT[:, kc, 0:tok], kc == 0, kc == 7, [k1] + U, [("ps", b)])
                dve((lambda e, o=ckvT[:, c, 0:tok], a=ps[:, b, 0:tok], s=kvg[:, c:c + 1]:
                     e.tensor_scalar(out=o, in0=a, scalar1=s, scalar2=None, op0=ALU.mult)),
                    [("ps", b)], K_ckv + [("ser",)])
                act(pT[c][:, 0:tok], ps[:, b, 0:tok], AF.Square, [("ps", b)], [("pT", c), ("ser",)])
            b = bank()
            for c in range(2):
                mm(ps[:, b, 0:tok], ones_bf[:, :], pT[c][:, 0:tok], c == 0, c == 1, [("pT", c)], [("ps", b)])
            act(rkv_rep[:, 0:tok], ps[:, b, 0:tok], AF.Sqrt, [("ps", b)], [("E", 1)], scale=1.0 / 256, bias=EPS)
            dve((lambda e: e.reciprocal(out=rkv_rep[:, 0:tok], in_=rkv_rep[:, 0:tok])), [("E", 1)], [("E", 1)])
            for i, (t0, r) in enumerate(tiles):
                b = bank()
                for c in range(2):
                    mm(ps[0:r, b, 0:1], pT[c][:, t0:t0 + r], ones_bf[:, 0:1], c == 0, c == 1, [("pT", c)], [("ps", b)])
                rs = stat[0:r, 4 + i:5 + i]
                act(rs, ps[0:r, b, 0:1], AF.Sqrt, [("ps", b)], [("st", 4 + i)], scale=1.0 / 256, bias=EPS)
                dve((lambda e, rs=rs: e.reciprocal(out=rs, in_=rs)), [("st", 4 + i)], [("st", 4 + i)])
            if PP <= 5:
                return
            pieces = []
            for rep in range(3):
                pieces.append((lambda s_, rep=rep: v3(s_, 8, 192)[:, :, rep * 32:(rep + 1) * 32], win_cols(2176, 32)))
                pieces.append((lambda s_, rep=rep: v3(s_, 8, 192)[:, :, 96 + rep * 32:96 + rep * 32 + 16], win_cols(2192, 16)))
                pieces.append((lambda s_, rep=rep: v3(s_, 8, 192)[:, :, 96 + rep * 32 + 16:96 + rep * 32 + 32], win_cols(2176, 16)))
            st, kk = wload(pieces)
            W = v3(st, 8, 192)
            braw = bank()
            for kc in range(8):
                mm(ps[0:96, braw, 0:tok], W[:, kc, 0:96], uT[:, kc, 0:tok], kc == 0, kc == 7, [kk] + U, [("ps", braw)])
            brot = bank()
            for kc in range(8):
                mm(ps[0:96, brot, 0:tok], W[:, kc, 96:192], uT[:, kc, 0:tok], kc == 0, kc == 7, [kk] + U, [("ps", brot)])
            dve((lambda e, braw=braw: e.tensor_tensor(out=rt1[:, 0:tok], in0=ps[0:96, braw, 0:tok], in1=cosT[:, 0:tok], op=ALU.mult)),
                [("ps", braw), ("cosT",)], [("Sfx", 0)])
            dve((lambda e, brot=brot: e.tensor_tensor(out=rt2[:, 0:tok], in0=ps[0:96, brot, 0:tok], in1=sinT[:, 0:tok], op=ALU.mult)),
                [("ps", brot), ("sinT",)], [("Sfx", 1)])
            dve((lambda e: e.tensor_tensor(out=kT_rope[:, kp:kp + tok], in0=rt1[:, 0:tok], in1=rt2[:, 0:tok], op=ALU.add)),
                [("Sfx", 0), ("Sfx", 1)], [("kT_rope", gi)])
            if PP <= 6:
                return
            if not is_meta:
                stn, kn = wload([(lambda s_, kc=kc: s_[:, kc * 512:(kc + 1) * 512].rearrange("p (h d) -> p h d", h=8),
                                  wuq_d[kc * 128:(kc + 1) * 128, :, 0:64]) for kc in range(3)])
                Wn = stn[:, 0:1536].rearrange("p (kc hd) -> p kc hd", kc=3)
                for c in range(4):
                    b = bank()
                    for kc in range(3):
                        mm(ps[:, b, 0:tok], Wn[:, kc, c * 128:(c + 1) * 128], cqT[:, kc, 0:tok], kc == 0, kc == 2,
                           [kn] + K_cq, [("ps", b)])
                    dve((lambda e, o=qm_nope[0:64, 2 * c, 0:tok], a=ps[0:64, b, 0:tok]:
                         e.tensor_tensor(out=o, in0=a, in1=rq_rep[0:64, 0:tok], op=ALU.mult)),
                        [("ps", b), ("E", 0)], [("qT_nope",)])
                    dve((lambda e, o=qm_nope[64:128, 2 * c + 1, 0:tok], a=ps[64:128, b, 0:tok]:
                         e.tensor_tensor(out=o, in0=a, in1=rq_rep[64:128, 0:tok], op=ALU.mult)),
                        [("ps", b), ("E", 0)], [("qT_nope",)])
                if PP <= 7:
                    return
                def rv(s_, base, kc):
                    return s_[:, base + kc * 256:base + (kc + 1) * 256].rearrange("p (h d) -> p h d", h=8)
                rp = []
                for kc in range(3):
                    rws = slice(kc * 128, (kc + 1) * 128)
                    rp.append((lambda s_, kc=kc: rv(s_, 0, kc), wuq_d[rws, :, 64:96]))
                    rp.append((lambda s_, kc=kc: rv(s_, 768, kc)[:, :, 0:16], wuq_d[rws, :, 80:96]))
                    rp.append((lambda s_, kc=kc: rv(s_, 768, kc)[:, :, 16:32], wuq_d[rws, :, 64:80]))
                str_, kr = wload(rp)
                Wr = str_[:, 0:768].rearrange("p (kc hd) -> p kc hd", kc=3)
                Wt = str_[:, 768:1536].rearrange("p (kc hd) -> p kc hd", kc=3)
                for gq in range(3):
                    h0 = gq * 3
                    nh_ = 3 if gq < 2 else 2
                    m = nh_ * 32
                    braw = bank()
                    for kc in range(3):
                        mm(ps[0:m, braw, 0:tok], Wr[:, kc, h0 * 32:h0 * 32 + m], cqT[:, kc, 0:tok], kc == 0, kc == 2,
                           [kr] + K_cq, [("ps", braw)])
                    brot = bank()
                    for kc in range(3):
                        mm(ps[0:m, brot, 0:tok], Wt[:, kc, h0 * 32:h0 * 32 + m], cqT[:, kc, 0:tok], kc == 0, kc == 2,
                           [kr] + K_cq, [("ps", brot)])
                    dve((lambda e, m=m, braw=braw: e.tensor_tensor(out=rt1[0:m, 0:tok], in0=ps[0:m, braw, 0:tok], in1=cosT[0:m, 0:tok], op=ALU.mult)),
                        [("ps", braw), ("cosT",)], [("Sfx", 0)])
                    dve((lambda e, m=m, brot=brot: e.tensor_tensor(out=rt2[0:m, 0:tok], in0=ps[0:m, brot, 0:tok], in1=sinT[0:m, 0:tok], op=ALU.mult)),
                        [("ps", brot), ("sinT",)], [("Sfx", 1)])
                    dve((lambda e, m=m: e.tensor_tensor(out=rt1[0:m, 0:tok], in0=rt1[0:m, 0:tok], in1=rt2[0:m, 0:tok], op=ALU.add)),
                        [("Sfx", 0), ("Sfx", 1)], [("Sfx", 0)])
                    for jh in range(nh_):
                        rr = slice(32 * jh, 32 * jh + 32)
                        dve((lambda e, rr=rr, hq=h0 + jh: e.tensor_tensor(out=qm_rope[rr, hq, 0:tok], in0=rt1[rr, 0:tok], in1=rq_rep[rr, 0:tok], op=ALU.mult)),
                            [("Sfx", 0), ("E", 0)], [("qT_rope",)])
            if PP <= 8:
                return
            stk, kkk = wload([(lambda s_, kc=kc: s_[:, kc * 512:(kc + 1) * 512].rearrange("p (h d) -> p h d", h=8),
                               wukv_d[kc * 128:(kc + 1) * 128, :, 0:64]) for kc in range(2)])
            Wk = stk[:, 0:1024].rearrange("p (kc hd) -> p kc hd", kc=2)
            for c in range(4):
                b = bank()
                for kc in range(2):
                    mm(ps[:, b, 0:tok], Wk[:, kc, c * 128:(c + 1) * 128], ckvT[:, kc, 0:tok], kc == 0, kc == 1,
                       [kkk] + K_ckv, [("ps", b)])
                dve((lambda e, o=kT_nope[:, c, kp:kp + tok], a=ps[:, b, 0:tok]:
                     e.tensor_tensor(out=o, in0=a, in1=rkv_rep[:, 0:tok], op=ALU.mult)),
                    [("ps", b), ("E", 1)], [("kT_nope", gi)])
            stv, kv_ = wload([(lambda s_, kc=kc: s_[:, kc * 512:(kc + 1) * 512].rearrange("p (h d) -> p h d", h=8),
                               wukv_d[kc * 128:(kc + 1) * 128, :, 64:128]) for kc in range(2)])
            Wv = stv[:, 0:1024].rearrange("p (kc hd) -> p kc hd", kc=2)
            for i, (t0, r) in enumerate(tiles):
                b = bank()
                for kc in range(2):
                    mm(ps[0:r, b, :], ckvT[:, kc, t0:t0 + r], Wv[:, kc, :], kc == 0, kc == 1, [kv_] + K_ckv, [("ps", b)])
                vb = 0 if is_meta else 1 + (g % GPS) * 4 + i
                act(v_mla[0:r, vb, :, 0:64], ps[0:r, b, :].rearrange("p (h d) -> p h d", h=8), AF.Copy,
                    [("ps", b), ("st", 4 + i)], [("v_mla", gi)], scale=stat[0:r, 4 + i:5 + i])

        def key_blocks(j):
            bl = [(0, NMETA, 0)]
            for m in range(j + 1):
                bl.append((NMETA + 128 * m, 128, 1 + m))
            return bl

        rotE = [0]
        rotS = [0]
        rotA = [0]
        rotT = [0]
        rotP = [0]
        rotQ = [0]
        NSTG = max(len(CFG["sb_stages"]), len(CFG["mla_stages"]))

        def run_pipeline(units):
            n = len(units)
            for k in range(n + NSTG - 1):
                if CFG["order"] == "rev":
                    sorder = list(range(NSTG - 1, -1, -1))
                elif CFG["order"] == "fwd":
                    sorder = list(range(NSTG))
                else:
                    sorder = [0] + list(range(NSTG - 1, 0, -1))
                for sidx in sorder:
                    u = k - sidx
                    if 0 <= u < n and units[u][sidx] is not None:
                        units[u][sidx]()

        def sb_unit(ctx, hd, ch, is_last_chunk, first_flag, last_flag, kres, vres, fin):
            t0 = ctx["t0"]
            c = hd // 2
            pb = 64 * (hd % 2)
            k0 = ch[0][0]
            kn = sum(w for (_, w, _) in ch)
            nb = len(ch)
            st = {}

            def sA():
                rE = rotE[0] % NE
                rotE[0] += 1
                rQ = rotQ[0] % NSP
                rotQ[0] += 1
                st["rE"] = rE
                st["rQ"] = rQ
                E = Ebuf[rE]
                SPv = SPbuf[rQ]
                if "Z" not in [n_ for g_ in CFG["sb_stages"] for n_ in g_]:
                    sZ()
                zb = st["zb"]
                act(E[:, 0:kn], ps[:, zb, 0:kn], AF.Exp, [("ps", zb)], [("E", rE)])
                held.discard(zb)
                act(SPv[:, 0:kn], E[:, 0:kn], AF.Ln, [("E", rE)], [("SP", rQ)], bias=1.0)

            def sZ():
                if ctx["ob"] is None:
                    ctx["ob"] = bank()
                    held.add(ctx["ob"])
                zb = bank()
                held.add(zb)
                st["zb"] = zb
                mm(ps[:, zb, 0:kn], qm_sb[:, hd, t0:t0 + 128], kT_sb[:, c, k0:k0 + kn], True, not is_last_chunk,
                   [("qT_sb",)] + kres, [("ps", zb)])
                if is_last_chunk:
                    mm(ps[:, zb, kn - 128:kn], ident[:, :], negm[:, :], False, True, [], [("ps", zb)])

            def sA2():
                rQ = st["rQ"]
                SPv = SPbuf[rQ]
                rS = rotS[0] % NS
                rotS[0] += 1
                st["rS"] = rS
                carry = ctx["carry"].get(hd)
                init = 0.0 if carry is None else carry[0]
                rd = [("SP", rQ)] + ([] if carry is None else [carry[1]])
                dve((lambda e: e.tensor_tensor_scan(
                    out=Sfx[rS][:, 0:kn][:, ::-1], data0=ones_f[:, 0:kn], data1=SPv[:, 0:kn][:, ::-1],
                    initial=init, op0=ALU.mult, op1=ALU.add)), rd, [("Sfx", rS)], cost=(2.25 * kn, 0.0))
                ctx["carry"][hd] = (Sfx[rS][:, 0:1], ("Sfx", rS))

            def sB():
                rE, rQ, rS = st["rE"], st["rQ"], st["rS"]
                E = Ebuf[rE]
                SPv = SPbuf[rQ]
                rA = rotA[0] % NA
                rotA[0] += 1
                st["rA"] = rA
                A = abuf[rA]
                act(SPv[:, 0:kn], Sfx[rS][:, 0:kn], AF.Exp, [("Sfx", rS)], [("SP", rQ)], scale=-1.0)
                S.add("pool", (lambda e: e.tensor_tensor(out=A[:, 0:kn], in0=E[:, 0:kn], in1=SPv[:, 0:kn], op=ALU.mult)),
                      [("E", rE), ("SP", rQ)], [("a", rA)], cost=(150.0 + 1.72 * kn, 0.0))

            def sC():
                rA = st["rA"]
                A = abuf[rA]
                tb = bank()
                held.add(tb)
                st["tb"] = tb
                pv = psbf(tb)
                off = 0
                for bi, (kcol, w, vb) in enumerate(ch):
                    tr(pv[0:w, bi * 128:(bi + 1) * 128], A[:, off:off + w], ident[:, :], [("a", rA)], [("ps", tb)])
                    off += w

            def sD():
                tb = st["tb"]
                rT = rotT[0] % NT_
                rotT[0] += 1
                st["rT"] = rT
                pv = psbf(tb)
                b0 = 0
                if ch[0][1] < 128:
                    w0 = ch[0][1]
                    dve((lambda e: e.tensor_copy(out=aT[rT][0:w0, 0, :], in_=pv[0:w0, 0:128])),
                        [("ps", tb)], [("aT", rT)], cost=(120.0, 0.0))
                    b0 = 1
                if nb > b0:
                    dve((lambda e: e.tensor_copy(out=aT[rT][:, b0:nb, :].rearrange("p a b -> p (a b)"), in_=pv[:, b0 * 128:nb * 128])),
                        [("ps", tb)], [("aT", rT)], cost=(60.0 + 0.48 * (nb - b0) * 128, 0.0))
                held.discard(tb)

            def sE():
                rT = st["rT"]
                ob = ctx["ob"]
                for bi, (kcol, w, vb) in enumerate(ch):
                    mm(ps[:, ob, hd * 64:(hd + 1) * 64], aT[rT][0:w, bi, :], v_sb[0:w, vb, hd * 64:(hd + 1) * 64],
                       first_flag and bi == 0, last_flag and bi == nb - 1, [("aT", rT)] + vres, [("ps", ob)])
                if fin:
                    dve((lambda e: e.tensor_copy(out=ytok[:, :], in_=ps[:, ob, :])), [("ps", ob)], [("ytok",)])
                    held.discard(ob)
                    tb = bank()
                    pv = psbf(tb)
                    for cc in range(4):
                        tr(pv[:, cc * 128:(cc + 1) * 128], ytok[:, cc * 128:(cc + 1) * 128], ident[:, :], [("ytok",)], [("ps", tb)])
                    act(actT[:, 8:12, t0:t0 + 128], pv[:, 0:512].rearrange("p (c t) -> p c t", c=4), AF.Copy, [("ps", tb)], K_ysb)

            prim = {"Z": sZ, "A": sA, "A2": sA2, "B": sB, "C": sC, "D": sD, "E": sE}

            def mk(names):
                def f():
                    for n_ in names:
                        Sched.curtag = f"sb{ctx['t0'] // 128}h{hd}k{k0}:{n_}"
                        prim[n_]()
                return f
            stg = [mk(g_) for g_ in CFG["sb_stages"]]
            return stg + [None] * (NSTG - len(stg))

        def mla_unit(cm, hd, blk, ib, is_first_blk, knres, krres, vres, fin, last_of_group):
            kcol, w, vb = blk
            c = hd // 2
            pb = 64 * (hd % 2)
            gq = hd // 3
            rb = 32 * (hd % 3)
            hl = hd % 2
            q4 = hd // 2
            t_lo = 0 if ib is None else ib
            N = (4 - t_lo) * 128
            tc0 = t_lo * 128
            st = {}

            def sA():
                if cm["obm"] is None:
                    cm["obm"] = [bank(), bank()]
                    for b_ in cm["obm"]:
                        held.add(b_)
                zb = bank()
                held.add(zb)
                st["zb"] = zb
                mm(ps[0:w, zb, 0:N], kT_nope[:, c, kcol:kcol + w], qm_nope[:, hd, tc0:G], True, False,
                   [("qT_nope",)] + knres, [("ps", zb)])
                mm(ps[0:w, zb, 0:N], kT_rope[:, kcol:kcol + w], qm_rope[:, hd, tc0:G], False, ib is None,
                   [("qT_rope",)] + krres, [("ps", zb)])
                if ib is not None:
                    mm(ps[:, zb, 0:128], ident[:, :], negm2[:, :], False, True, [], [("ps", zb)])

            def sB():
                zb = st["zb"]
                r = rotP[0] % NP
                rotP[0] += 1
                st["r"] = r
                P = pT[r]
                act(P[0:w, 0:N], ps[0:w, zb, 0:N], AF.Exp, [("ps", zb)], [("pT", r)], scale=MLA_SCALE)
                held.discard(zb)

            def sC():
                r = st["r"]
                P = pT[r]
                obm = cm["obm"]
                for i in range(t_lo, 4):
                    off = (i - t_lo) * 128
                    ob = obm[i // 2]
                    oc = (i % 2) * 130 + hl * 65
                    mm(ps[:, ob, oc:oc + 65], P[0:w, off:off + 128], v_mla[0:w, vb, hd, :],
                       is_first_blk and i % 2 == 0, (ib is not None and i == ib), [("pT", r)] + vres, [("ps", ob)], skip=True)
                if fin:
                    for i in range(4):
                        ob = obm[i // 2]
                        base = (i % 2) * 130
                        o3 = ps[:, ob, base:base + 130].rearrange("p (h d) -> p h d", h=2)
                        dve((lambda e, o3=o3, i=i: e.reciprocal(out=rc[:, 2 * i:2 * i + 2].unsqueeze(2), in_=o3[:, :, 64:65])),
                            [("ps", ob)], [("rc", i)])
                        dve((lambda e, o3=o3, i=i: e.tensor_tensor(
                            out=ytok2[:, i * 128:(i + 1) * 128].rearrange("p (h d) -> p h d", h=2), in0=o3[:, :, 0:64],
                            in1=rc[:, 2 * i:2 * i + 2].unsqueeze(2).to_broadcast([128, 2, 64]), op=ALU.mult)),
                            [("ps", ob), ("rc", i)], [("ytok2", i)])
                    if last_of_group:
                        for b_ in obm:
                            held.discard(b_)
                        cm["obm"] = None
                    tb = bank()
                    pv = psbf(tb)
                    for i in range(4):
                        tr(pv[:, i * 128:(i + 1) * 128], ytok2[:, i * 128:(i + 1) * 128], ident[:, :], [("ytok2", i)], [("ps", tb)])
                    act(actT[:, 12 + q4, :], pv[:, 0:512], AF.Copy, [("ps", tb)], [("actT", 12 + q4)])

            prim = {"A": sA, "B": sB, "C": sC}

            def mk(names):
                def f():
                    for n_ in names:
                        Sched.curtag = f"mla_h{hd}b{vb}:{n_}"
                        prim[n_]()
                return f
            stg = [mk(g_) for g_ in CFG["mla_stages"]]
            return stg + [None] * (NSTG - len(stg))

        def stage_attn(g):
            COLD[0] = False
            try:
                _stage_attn(g)
            finally:
                COLD[0] = False

        def _stage_attn(g):
            gs_ = g % GPS
            gl = ["m"] + list(range(gs_ + 1))
            kres = [("kT_sb", q) for q in gl]
            vres = [("v_sb", q) for q in gl]
            knres = [("kT_nope", q) for q in gl]
            krres = [("kT_rope", q) for q in gl]
            vmres = [("v_mla", q) for q in gl]
            su = []
            for i in range(4):
                j = gs_ * 4 + i
                blocks = key_blocks(j)
                chunks = [blocks[0:4]]
                rest = blocks[4:]
                while rest:
                    chunks.append(rest[0:4])
                    rest = rest[4:]
                ctx = {"t0": i * 128, "ob": None, "carry": {}}
                nch = len(chunks)
                for hd in range(8):
                    for q_, ci in enumerate(range(nch - 1, -1, -1)):
                        su.append(sb_unit(ctx, hd, chunks[ci], ci == nch - 1, q_ == 0, q_ == nch - 1, kres, vres,
                                          hd == 7 and q_ == nch - 1))
            allb = key_blocks(gs_ * 4 + 3)
            mu = []
            cm = {"obm": None}
            for hd in range(8):
                for bi_, blk in enumerate(allb):
                    xb = blk[2] - 1
                    ib = xb - gs_ * 4 if xb >= gs_ * 4 else None
                    lastb = bi_ == len(allb) - 1
                    mu.append(mla_unit(cm, hd, blk, ib, bi_ == 0, knres, krres, vmres,
                                       hd % 2 == 1 and lastb, hd == 7 and lastb))
            units = []
            a = b = 0
            while a < len(su) or b < len(mu):
                if b >= len(mu) or (a < len(su) and a * len(mu) <= b * len(su)):
                    units.append(su[a])
                    a += 1
                else:
                    units.append(mu[b])
                    b += 1
            run_pipeline(units)

        def stage_mixout(tiles, tok):
            load_gpost(1)
            for cp in range(4):
                c0 = cp * 256
                so, ko = wload([
                    (lambda s_: v3(s_, 8, 256)[:, 0:4, :], wsbo_d[:, c0:c0 + 256].rearrange("(kc p) n -> p kc n", p=128)),
                    (lambda s_: v3(s_, 8, 256)[:, 4:8, :], wmlao_d[:, c0:c0 + 256].rearrange("(kc p) n -> p kc n", p=128)),
                ])
                WO = v3(so, 8, 256)
                s1, k1 = wload([(lambda s_: v3(s_, 8, 256), win_cols(2208 + c0, 256))])
                s2, k2 = wload([(lambda s_: v3(s_, 8, 256), win_cols(2208 + 1024 + c0, 256))])
                W1 = v3(s1, 8, 256)
                W2 = v3(s2, 8, 256)
                for jj in range(2):
                    c = 2 * cp + jj
                    cs = slice(jj * 128, (jj + 1) * 128)
                    b1 = bank()
                    for kc in range(4):
                        mm(ps[:, b1, 0:tok], WO[:, kc, cs], yT_sb[:, kc, 0:tok], kc == 0, kc == 3, [ko] + K_ysb, [("ps", b1)])
                    b2 = bank()
                    for kc in range(4):
                        mm(ps[:, b2, 0:tok], WO[:, 4 + kc, cs], yT_mla[:, kc, 0:tok], kc == 0, kc == 3, [ko] + K_ymla, [("ps", b2)])
                    b3 = bank()
                    for kc in range(8):
                        mm(ps[:, b3, 0:tok], W1[:, kc, cs], uT[:, kc, 0:tok], kc == 0, kc == 7, [k1, ("uT",)], [("ps", b3)])
                    b4 = bank()
                    for kc in range(8):
                        mm(ps[:, b4, 0:tok], W2[:, kc, cs], uT[:, kc, 0:tok], kc == 0, kc == 7, [k2, ("uT",)], [("ps", b4)])
                    act(gs[:, 0:tok], ps[:, b3, 0:tok], AF.Sigmoid, [("ps", b3)], [("pT", 0)], bias=bg[:, c:c + 1])
                    act(gm[:, 0:tok], ps[:, b4, 0:tok], AF.Sigmoid, [("ps", b4)], [("pT", 1)], bias=bg[:, 8 + c:9 + c])
                    dve((lambda e, b1=b1: e.tensor_tensor(out=mt1[:, 0:tok], in0=ps[:, b1, 0:tok], in1=gs[:, 0:tok], op=ALU.mult)),
                        [("ps", b1), ("pT", 0)], [("E", 0)])
                    dve((lambda e, b2=b2: e.tensor_tensor(out=mt2[:, 0:tok], in0=ps[:, b2, 0:tok], in1=gm[:, 0:tok], op=ALU.mult)),
                        [("ps", b2), ("pT", 1)], [("E", 1)])
                    dve((lambda e, c=c: e.tensor_tensor(out=mixT[:, c, 0:tok], in0=mt1[:, 0:tok], in1=mt2[:, 0:tok], op=ALU.add)),
                        [("E", 0), ("E", 1)], [("actT", c)])
            ws = []
            for q in range(4):
                st, kk = wload([(lambda s_: v3(s_, 2, D), wout_d[q * 256:(q + 1) * 256, :].rearrange("(j p) n -> p j n", p=128))])
                ws.append((v3(st, 2, D), kk))
            bank_ctr[0] = 0
            for i, (t0, r) in enumerate(tiles):
                for nh in range(2):
                    b = 2 * i + nh
                    for kc in range(8):
                        W, kk = ws[kc // 2]
                        mm(ps[0:r, b, :], mixT[:, kc, t0:t0 + r], W[:, kc % 2, nh * 512:(nh + 1) * 512], kc == 0, kc == 7,
                           [kk, ("actT", kc)], [("ps", b)])
            stage_post(tiles, 1, half=False)
            bank_ctr[0] = 2 * len(tiles)

        def stage_store(g):
            ops_ = []
            for i in range(4):
                r0 = g * G + i * 128
                ops_.append(S.add("sp", (lambda e, i=i, r0=r0: e.dma_start(out=out_d[r0:r0 + 128, :], in_=h[:, i, :])),
                                  reads=[("h", i)], writes=[("out", g, i)], dma=f"ho{i}", cost=(100.0, 524288.0)))
            return ops_

        MT = [(0, NMETA)]
        GT = [(i * 128, 128) for i in range(4)]
        prog = []
        prog += [("load", -1), ("rope", -1), ("norm", MT, 0), ("ffn", MT, 0, NMETA), ("norm", MT, 1), ("proj", MT, NMETA, -1)]
        for g in range(NG):
            prog += [("load", g), ("rope", g), ("norm", GT, 0), ("ffn", GT, 0, G), ("norm", GT, 1), ("proj", GT, G, g),
                     ("attn", g), ("mixout", GT, G), ("norm", GT, 2), ("ffn", GT, 1, G), ("store", g)]
        if dbg is not None and "simattn" in dbg:
            COLD[0] = False
            stage_attn(dbg["simattn"])
            res = S.simulate()
            if dbg.get("crit"):
                allops = [o for e_ in S.ENGS for o in S.ops[e_]]
                last = max(allops, key=lambda o: S.sim_done[id(o)])
                import collections
                cat = collections.Counter()
                op = last
                while op is not None:
                    kind, p = S.sim_pred[id(op)]
                    cat[(op.eng, kind)] += 1
                    dur = S.sim_done[id(op)] - S.sim_start[id(op)]
                    cat[("time", op.eng)] += dur
                    op = p
                print(sorted(cat.items(), key=lambda kv: -kv[1])[:14])
                op = last
                path = []
                while op is not None and len(path) < 400:
                    kind, p = S.sim_pred[id(op)]
                    path.append(f"{S.sim_start[id(op)]/1e3:8.2f} {op.eng:4s} {op.tag:22s} {kind} dur={S.sim_done[id(op)]-S.sim_start[id(op)]:.0f}")
                    op = p
                for l_ in path[200:260][::-1]:
                    print(l_)
            return res
        if dbg is not None:
            prog = prog[:dbg["nstage"]]
        fns = {"load": stage_load, "rope": stage_rope_tab, "norm": stage_norm, "ffn": stage_ffn, "proj": stage_proj,
               "attn": stage_attn, "mixout": stage_mixout, "store": stage_store}
        stored = []
        for st_ in prog:
            fns[st_[0]](*st_[1:])
            if st_[0] == "store":
                stored.append(st_[1])
        fin = S.add("sp", None, reads=[("out", g, i) for g in stored for i in range(4)])
        if dbg is not None and dbg.get("simall"):
            res = S.simulate()
            return res[0], res[1], S.dma_busy
        S.finalize()

        esem = {e_: es.enter_context(nc.semaphore(f"es_{e_}")) for e_ in Sched.ENGS}
        dsem = {n_: es.enter_context(nc.semaphore(f"ds_{n_}")) for n_ in S.dmacnt}
        with nc.Block() as block:
            @block.tensor
            def _(e):
                S.run(e, "pe", esem, dsem)

            @block.scalar
            def _(e):
                S.run(e, "act", esem, dsem)

            @block.vector
            def _(e):
                S.run(e, "dve", esem, dsem)

            @block.gpsimd
            def _(e):
                S.run(e, "pool", esem, dsem)

            @block.sync
            def _(e):
                S.run(e, "sp", esem, dsem)
        if dbg is not None and dbg.get("dump"):
            T = dict(h=h, uT=uT, actT=actT, kT_sb=kT_sb, v_sb=v_sb, kT_nope=kT_nope, kT_rope=kT_rope, v_mla=v_mla,
                     qT_sb=qm_sb, qT_nope=qm_nope, qT_rope=qm_rope, xn=xn, stat=stat,
                     gpre=gpre, bg=bg, cosT=cosT, sinT=sinT, EB=EB, ytok=ytok)
            dsm = es.enter_context(nc.semaphore("dumpsem"))
            with nc.Block() as block:
                @block.gpsimd
                def _(e):
                    n = 0
                    for name in dbg["dump"]:
                        t = T[name]
                        shp = list(t.shape)
                        dd = nc.dram_tensor("dbg_" + name, shp, F32, kind="ExternalOutput").ap()
                        e.dma_start(out=dd, in_=t[:]).then_inc(dsm, 16)
                        n += 16
                    e.wait_ge(dsm, n)
    return nc


def _rope_table():
    half = 16
    inv = (10000.0 ** (-np.arange(half, dtype=np.float32) / half)).astype(np.float32)
    pos = np.arange(KLEN, dtype=np.float32)
    ang = pos[None, :] * inv[:, None]
    cos = np.cos(ang).astype(np.float32)
    sin = np.sin(ang).astype(np.float32)
    c32 = np.concatenate([cos, cos], axis=0)
    s32 = np.concatenate([-sin, sin], axis=0)
    tab = np.stack([np.tile(c32, (3, 1)), np.tile(s32, (3, 1))], axis=0)
    return np.ascontiguousarray(tab, dtype=np.float32)


def kernel(**inputs):
    f = lambda a: np.ascontiguousarray(np.asarray(a), dtype=np.float32)
    x = f(inputs["x"])
    shared = {
        "meta": f(inputs["meta_tokens"]),
        "gains_pre": np.ascontiguousarray(np.concatenate(
            [f(inputs["ffn1_pre_g"]), f(inputs["mix_pre_g"]), f(inputs["ffn2_pre_g"])], axis=0)),
        "gains_post": np.ascontiguousarray(np.concatenate(
            [f(inputs["ffn1_post_g"]), f(inputs["mix_post_g"]), f(inputs["ffn2_post_g"])], axis=1)),
        "ffn1_w_gate": f(inputs["ffn1_w_gate"])[0], "ffn2_w_gate": f(inputs["ffn2_w_gate"])[0],
        "ffn1_w_up": f(inputs["ffn1_w_up"])[0], "ffn2_w_up": f(inputs["ffn2_w_up"])[0],
        "ffn1_w_down": f(inputs["ffn1_w_down"])[0], "ffn2_w_down": f(inputs["ffn2_w_down"])[0],
        "w_in": f(inputs["w_in"])[0],
        "b_gate": f(inputs["b_gate"]),
        "q_norm_g": f(inputs["q_norm_g"]),
        "kv_norm_g": f(inputs["kv_norm_g"]),
        "w_uq": f(inputs["w_uq"])[0],
        "w_ukv": f(inputs["w_ukv"])[0],
        "w_sb_o": f(inputs["w_sb_o"])[0],
        "w_mla_o": f(inputs["w_mla_o"])[0],
        "w_out": f(inputs["w_out"])[0],
        "rope_tab": _rope_table(),
    }
    nc = build_nc()
    in_maps = []
    for c in range(NCORES):
        m = dict(shared)
        m["x"] = np.ascontiguousarray(x[c * SEQ_PER_CORE:(c + 1) * SEQ_PER_CORE].reshape(XTOK, D))
        in_maps.append(m)
    res = run_bass_kernel_spmd(nc, in_maps, core_ids=list(range(NCORES)))
    out = np.stack([np.asarray(r["out"], dtype=np.float32).reshape(SEQ_PER_CORE, SEQ, D) for r in res.results], axis=0)
    return out.reshape(NCORES * SEQ_PER_CORE, SEQ, D)
```
